# Optimizing a Trainium2 kernel written in Bass

```python
import math
import jax, jax.numpy as jnp
from jax import lax
import numpy as np

D_MODEL = 2048
BATCH = 2
SEQ = 4096
DEPTH = 2
DEC_BATCH = 8
DEC_SEQ = 1
PAST_LEN = 16384
PAGE_SIZE = 128

HEAD_DIM = 128
MIX_WIDTH = D_MODEL
N_MEM = 256
N_MEM_HEADS = 4
MEM_WIDTH = N_MEM_HEADS * HEAD_DIM
MIXER_WIDTH = MIX_WIDTH - MEM_WIDTH
CHUNK = 128
N_GROUPS_A = 4
GROUP_DIM_A = MIXER_WIDTH // N_GROUPS_A
SWA_PATTERN = ((128, 1), (512, 4), (2048, 16))
N_SWA_GROUPS = 3
HEADS_PER_GROUP = MIXER_WIDTH // (HEAD_DIM * N_SWA_GROUPS)
N_HEADS_B = HEADS_PER_GROUP * N_SWA_GROUPS
N_BUCKETS = 32
MAX_EXACT = N_BUCKETS // 2
MAX_DISTANCE = 2048
D_FF = -(-8 * D_MODEL // (3 * 256)) * 256
N_MIXERS = 2
N_GMLP_LAYERS = (DEPTH + 1) // 2
N_SWA_LAYERS = DEPTH // 2
EPS = 1e-6
NEG_INF = -1e30

kernel_name = 'hybrid_gmlp_dilated_swa_decode_step'


def rms_norm(x, g):
    xf = x.astype(jnp.float32)
    y = xf * lax.rsqrt(jnp.mean(xf * xf, axis=-1, keepdims=True) + EPS)
    return (y * g.astype(jnp.float32)).astype(x.dtype)


def t5_bucket(dist):
    nf = jnp.maximum(dist, MAX_EXACT).astype(jnp.float32)
    large = MAX_EXACT + (jnp.log(nf / MAX_EXACT) / math.log(MAX_DISTANCE / MAX_EXACT) * (N_BUCKETS - MAX_EXACT)).astype(jnp.int32)
    large = jnp.minimum(large, N_BUCKETS - 1)
    return jnp.where(dist < MAX_EXACT, dist, large)


def group_bias(rel_bias, g, dil, n_back):
    dist = jnp.arange(n_back + 1, dtype=jnp.int32) * dil
    b = rel_bias[t5_bucket(dist)][:, g * HEADS_PER_GROUP:(g + 1) * HEADS_PER_GROUP]
    return b.T.astype(jnp.float32)


def spatial_gating(z, g_v, w_s, b_s):
    B, T, _ = z.shape
    uv = jax.nn.gelu(z[..., :2 * MIXER_WIDTH])
    u, v = uv[..., :MIXER_WIDTH], uv[..., MIXER_WIDTH:]
    v = rms_norm(v, g_v)
    n_chunks = -(-T // CHUNK)
    vp = jnp.pad(v, ((0, 0), (0, n_chunks * CHUNK - T), (0, 0)))
    vp = vp.reshape(B, n_chunks, CHUNK, N_GROUPS_A, GROUP_DIM_A)
    w = w_s * jnp.tril(jnp.ones((CHUNK, CHUNK), w_s.dtype))
    s = jnp.einsum('gij,bnjgc->bnigc', w, vp) + b_s.T[None, None, :, :, None]
    s = s.reshape(B, n_chunks * CHUNK, MIXER_WIDTH)[:, :T]
    return u * s, v


def memory_kv(mem, g_mem, w_mem_kv):
    B = mem.shape[0]
    return (rms_norm(mem, g_mem) @ w_mem_kv).reshape(B, N_MEM, 2, N_MEM_HEADS, HEAD_DIM)


def memory_attention(q_cols, kv):
    B, T, _ = q_cols.shape
    q = q_cols.reshape(B, T, N_MEM_HEADS, HEAD_DIM)
    s = jnp.einsum('bthd,bmhd->bhtm', q, kv[:, :, 0]).astype(jnp.float32) * (HEAD_DIM ** -0.5)
    p = jax.nn.softmax(s, axis=-1).astype(q.dtype)
    return jnp.einsum('bhtm,bmhd->bthd', p, kv[:, :, 1]).reshape(B, T, MEM_WIDTH)


def swa_qkv(z):
    B, T, _ = z.shape
    qkv = z[..., :3 * MIXER_WIDTH].reshape(B, T, 3, N_HEADS_B, HEAD_DIM)
    return qkv[:, :, 0], qkv[:, :, 1], qkv[:, :, 2]


def dilated_attention_prompt(q, k, v, dil, n_back, bias_j):
    B, T, H, Dh = q.shape
    L = T // dil
    blk = n_back
    n_blk = -(-L // blk)
    Lp = n_blk * blk

    def to_sub(a):
        a = a.reshape(B, L, dil, H, Dh).transpose(0, 2, 1, 3, 4)
        return jnp.pad(a, ((0, 0), (0, 0), (0, Lp - L), (0, 0), (0, 0)))

    def band(a):
        a = jnp.pad(a, ((0, 0), (0, 0), (blk, 0), (0, 0), (0, 0))).reshape(B, dil, n_blk + 1, blk, H, Dh)
        return jnp.concatenate([a[:, :, :-1], a[:, :, 1:]], axis=3)

    qb = to_sub(q).reshape(B, dil, n_blk, blk, H, Dh)
    kb, vb = band(to_sub(k)), band(to_sub(v))
    qi = jnp.arange(blk)[:, None]
    kj = jnp.arange(2 * blk)[None, :]
    j = qi + blk - kj
    key_idx = jnp.arange(n_blk)[:, None, None] * blk - blk + kj[None]
    valid = (j >= 0) & (j <= n_back) & (key_idx >= 0)
    bias = bias_j[:, jnp.clip(j, 0, n_back)]
    s = jnp.einsum('brnqhd,brnkhd->brnhqk', qb, kb).astype(jnp.float32) * (Dh ** -0.5) + bias
    s = jnp.where(valid[None, None, :, None], s, NEG_INF)
    m = jnp.max(s, axis=-1, keepdims=True)
    p = jnp.exp(s - m)
    den = jnp.sum(p, axis=-1, keepdims=True)
    o = jnp.einsum('brnhqk,brnkhd->brnqhd', (p / den).astype(v.dtype), vb)
    lse = (m + jnp.log(den))[..., 0]
    o = o.reshape(B, dil, Lp, H, Dh)[:, :, :L].transpose(0, 2, 1, 3, 4).reshape(B, T, H, Dh)
    lse = lse.transpose(0, 1, 2, 4, 3).reshape(B, dil, Lp, H)[:, :, :L].transpose(0, 2, 1, 3).reshape(B, T, H)
    return o, lse


def dilated_attention_sample(q, k_all, v_all, n_buf, dil, n_back, bias_j):
    S = q.shape[1]
    idx = n_buf + jnp.arange(S)[:, None] - jnp.arange(n_back + 1)[None, :] * dil
    valid = idx >= 0
    idx = jnp.maximum(idx, 0)
    kg, vg = k_all[:, idx], v_all[:, idx]
    s = jnp.einsum('bshd,bsjhd->bhsj', q, kg).astype(jnp.float32) * (q.shape[-1] ** -0.5) + bias_j[None, :, None, :]
    s = jnp.where(valid[None, None], s, NEG_INF)
    lse = jax.nn.logsumexp(s, axis=-1)
    p = jnp.exp(s - lse[..., None]).astype(v_all.dtype)
    o = jnp.einsum('bhsj,bsjhd->bshd', p, vg)
    return o, lse.transpose(0, 2, 1)


def merge_groups(outs, lses):
    alpha = jax.nn.softmax(jnp.stack(lses, axis=0), axis=0)
    o = jnp.stack(outs, axis=0) * alpha[..., None].astype(outs[0].dtype)
    G, B, T, H, Dh = o.shape
    return o.transpose(1, 2, 0, 3, 4).reshape(B, T, G * H * Dh)


def swa_prompt(z, bias_groups):
    q, k, v = swa_qkv(z)
    T = z.shape[1]
    outs, lses, rows = [], [], []
    for g, (win, dil) in enumerate(SWA_PATTERN):
        hs = slice(g * HEADS_PER_GROUP, (g + 1) * HEADS_PER_GROUP)
        qg, kg, vg = q[:, :, hs], k[:, :, hs], v[:, :, hs]
        o, lse = dilated_attention_prompt(qg, kg, vg, dil, win // dil, bias_groups[g])
        outs.append(o)
        lses.append(lse)
        n_keep = min(win, T)
        rows.append(jnp.stack([kg, vg], axis=2)[:, T - n_keep:])
    return merge_groups(outs, lses), rows


def swa_sample(z, bufs, bias_groups):
    q, k, v = swa_qkv(z)
    outs, lses, rows = [], [], []
    for g, (win, dil) in enumerate(SWA_PATTERN):
        hs = slice(g * HEADS_PER_GROUP, (g + 1) * HEADS_PER_GROUP)
        qg, kg, vg = q[:, :, hs], k[:, :, hs], v[:, :, hs]
        buf = bufs[g]
        k_all = jnp.concatenate([buf[:, :, 0], kg], axis=1)
        v_all = jnp.concatenate([buf[:, :, 1], vg], axis=1)
        o, lse = dilated_attention_sample(qg, k_all, v_all, buf.shape[1], dil, win // dil, bias_groups[g])
        outs.append(o)
        lses.append(lse)
        rows.append(jnp.stack([kg, vg], axis=2))
    return merge_groups(outs, lses), rows


def mix_residual(x, mixer_out, q_mem, mem_kv, w_out, g_post):
    o = jnp.concatenate([mixer_out, memory_attention(q_mem, mem_kv)], axis=-1) @ w_out
    return x + rms_norm(o, g_post)


def ffn_residual(x, g_pre, g_post, w_up, w_down):
    h = rms_norm(x, g_pre) @ w_up
    return x + rms_norm((jax.nn.silu(h[..., :D_FF]) * h[..., D_FF:]) @ w_down, g_post)


def setup_inputs(seed: int = 0) -> dict:
    key = jax.random.key(seed)
    ks = jax.random.split(key, 24)
    f32 = jnp.float32

    def nrm(k, shape, scale):
        return jax.random.normal(k, shape, f32) * scale

    def gain(k, shape):
        return 1.0 + 0.05 * jax.random.normal(k, shape, f32)

    win_rows = [min(w, PAST_LEN) for w, _ in SWA_PATTERN]
    kv_tail = (2, HEADS_PER_GROUP, HEAD_DIM)
    return {
        'x_prompt': nrm(ks[0], (BATCH, SEQ, D_MODEL), 1.0),
        'x_sample': nrm(ks[1], (DEC_BATCH, DEC_SEQ, D_MODEL), 1.0),
        'mem_prompt': nrm(ks[2], (BATCH, N_MEM, D_MODEL), 1.0),
        'cache_mem_kv': nrm(ks[3], (DEPTH, DEC_BATCH, N_MEM, 2, N_MEM_HEADS, HEAD_DIM), 1.0),
        'cache_win128_kv': nrm(ks[4], (N_SWA_LAYERS, DEC_BATCH, win_rows[0]) + kv_tail, 1.0),
        'cache_win512_kv': nrm(ks[5], (N_SWA_LAYERS, DEC_BATCH, win_rows[1]) + kv_tail, 1.0),
        'cache_win2048_kv': nrm(ks[6], (N_SWA_LAYERS, DEC_BATCH, win_rows[2]) + kv_tail, 1.0),
        'rel_bias': nrm(ks[7], (N_BUCKETS, N_HEADS_B), 0.5),
        'norm_mix_pre': gain(ks[8], (DEPTH, D_MODEL)),
        'norm_mix_post': gain(ks[9], (DEPTH, D_MODEL)),
        'norm_ffn_pre': gain(ks[10], (DEPTH, D_MODEL)),
        'norm_ffn_post': gain(ks[11], (DEPTH, D_MODEL)),
        'norm_mem': gain(ks[12], (DEPTH, D_MODEL)),
        'w_mem_kv': nrm(ks[13], (DEPTH, D_MODEL, 2 * MEM_WIDTH), D_MODEL ** -0.5),
        'w_in_a': nrm(ks[14], (N_GMLP_LAYERS, D_MODEL, 2 * MIXER_WIDTH + MEM_WIDTH), D_MODEL ** -0.5),
        'norm_v_a': gain(ks[15], (N_GMLP_LAYERS, MIXER_WIDTH)),
        'w_spatial_a': nrm(ks[16], (N_GMLP_LAYERS, N_GROUPS_A, CHUNK, CHUNK), CHUNK ** -0.5),
        'b_spatial_a': 1.0 + 0.1 * jax.random.normal(ks[17], (N_GMLP_LAYERS, N_GROUPS_A, CHUNK), f32),
        'w_in_b': nrm(ks[18], (N_SWA_LAYERS, D_MODEL, 3 * MIXER_WIDTH + MEM_WIDTH), D_MODEL ** -0.5),
        'w_out': nrm(ks[19], (DEPTH, MIX_WIDTH, D_MODEL), MIX_WIDTH ** -0.5),
        'w_ffn_up': nrm(ks[20], (DEPTH, D_MODEL, 2 * D_FF), D_MODEL ** -0.5),
        'w_ffn_down': nrm(ks[21], (DEPTH, D_FF, D_MODEL), D_FF ** -0.5),
    }


def reference(x_prompt, x_sample, mem_prompt, cache_mem_kv, cache_win128_kv, cache_win512_kv,
              cache_win2048_kv, rel_bias, norm_mix_pre, norm_mix_post, norm_ffn_pre, norm_ffn_post,
              norm_mem, w_mem_kv, w_in_a, norm_v_a, w_spatial_a, b_spatial_a, w_in_b, w_out,
              w_ffn_up, w_ffn_down):
    bias_groups = [group_bias(rel_bias, g, dil, win // dil) for g, (win, dil) in enumerate(SWA_PATTERN)]
    win_caches = (cache_win128_kv, cache_win512_kv, cache_win2048_kv)
    yp, ys = x_prompt, x_sample
    mem_kv_p, chunk_v_s = [], []
    win_p = [[] for _ in SWA_PATTERN]
    win_s = [[] for _ in SWA_PATTERN]
    for i in range(DEPTH):
        li = i // N_MIXERS
        kv_p = memory_kv(mem_prompt, norm_mem[i], w_mem_kv[i])
        kv_s = cache_mem_kv[i]
        mem_kv_p.append(kv_p)
        hp = rms_norm(yp, norm_mix_pre[i])
        hs = rms_norm(ys, norm_mix_pre[i])
        if i % N_MIXERS == 0:
            zp = hp @ w_in_a[li]
            zs = hs @ w_in_a[li]
            mp, _ = spatial_gating(zp, norm_v_a[li], w_spatial_a[li], b_spatial_a[li])
            ms, v_rows = spatial_gating(zs, norm_v_a[li], w_spatial_a[li], b_spatial_a[li])
            chunk_v_s.append(v_rows)
            qp, qs = zp[..., 2 * MIXER_WIDTH:], zs[..., 2 * MIXER_WIDTH:]
        else:
            zp = hp @ w_in_b[li]
            zs = hs @ w_in_b[li]
            mp, rows_p = swa_prompt(zp, bias_groups)
            ms, rows_s = swa_sample(zs, [c[li] for c in win_caches], bias_groups)
            for g in range(N_SWA_GROUPS):
                win_p[g].append(rows_p[g])
                win_s[g].append(rows_s[g])
            qp, qs = zp[..., 3 * MIXER_WIDTH:], zs[..., 3 * MIXER_WIDTH:]
        yp = mix_residual(yp, mp, qp, kv_p, w_out[i], norm_mix_post[i])
        ys = mix_residual(ys, ms, qs, kv_s, w_out[i], norm_mix_post[i])
        yp = ffn_residual(yp, norm_ffn_pre[i], norm_ffn_post[i], w_ffn_up[i], w_ffn_down[i])
        ys = ffn_residual(ys, norm_ffn_pre[i], norm_ffn_post[i], w_ffn_up[i], w_ffn_down[i])
    new_mem_kv_prompt = jnp.stack(mem_kv_p, axis=0)
    new_chunk_v_sample = jnp.stack(chunk_v_s, axis=0)
    new_win128_kv_prompt = jnp.stack(win_p[0], axis=0)
    new_win512_kv_prompt = jnp.stack(win_p[1], axis=0)
    new_win2048_kv_prompt = jnp.stack(win_p[2], axis=0)
    new_win128_kv_sample = jnp.stack(win_s[0], axis=0)
    new_win512_kv_sample = jnp.stack(win_s[1], axis=0)
    new_win2048_kv_sample = jnp.stack(win_s[2], axis=0)
    return (yp, ys, new_mem_kv_prompt, new_chunk_v_sample,
            new_win128_kv_prompt, new_win512_kv_prompt, new_win2048_kv_prompt,
            new_win128_kv_sample, new_win512_kv_sample, new_win2048_kv_sample)
```

```python
import numpy as np
from contextlib import ExitStack
import concourse.bass as bass
import concourse.mybir as mybir
from concourse.bass_utils import run_bass_kernel_spmd

F32 = mybir.dt.float32
BF16 = mybir.dt.bfloat16
U8 = mybir.dt.uint8
AF = mybir.ActivationFunctionType
ALU = mybir.AluOpType
AX = mybir.AxisListType

D = 2048
KC = 16
NPR = 1024
NT = 1025
NTP = 1032
TB = [(0, 342), (342, 342), (684, 341)]
MIXW = 1536
DFF = 5632
FC = 44
SCALE = float(128 ** -0.5)
EPS = 1e-6
NG = 172
GELU_C = 1.5957691216057308


class Tok:
    __slots__ = ("sem", "val")

    def __init__(self, sem, val):
        self.sem = sem
        self.val = val


class Sched:
    ENG = ("pe", "act", "dve", "pool", "sp")

    def __init__(self, nc, stack):
        self.nc = nc
        self.stack = stack
        self.ops = {e: [] for e in self.ENG}
        self.prog = {e: stack.enter_context(nc.semaphore("prog_" + e)) for e in self.ENG}
        self.cnt = {e: 0 for e in self.ENG}
        self.seen = {e: {} for e in self.ENG}
        self.last = {e: None for e in self.ENG}
        self.dcnt = {}
        self.pending = []
        self.nsem = 0

    def new_sem(self, name=None):
        self.nsem += 1
        return self.stack.enter_context(self.nc.semaphore(name or f"s{self.nsem}"))

    def _flat(self, deps, acc):
        for d in deps:
            if d is None:
                continue
            if isinstance(d, (list, tuple)):
                self._flat(d, acc)
            else:
                k = id(d.sem)
                if k not in acc or acc[k].val < d.val:
                    acc[k] = d
        return acc

    def _waits(self, eng, deps, out=None):
        out = []
        for k, d in self._flat(deps, {}).items():
            if self.seen[eng].get(k, -1) >= d.val:
                continue
            self.seen[eng][k] = d.val
            out.append((d.sem, d.val))
        return out

    def op(self, eng, name, kw, deps=(), sig=None):
        if sig is None:
            sig = eng != "pe"
        fn = (name, kw)
        waits = self._waits(eng, deps)
        tok = None
        if sig:
            self.cnt[eng] += 1
            tok = Tok(self.prog[eng], self.cnt[eng])
            self.last[eng] = tok
        self.ops[eng].append((fn, waits, (self.prog[eng], 1) if sig else None))
        return tok

    def dma(self, eng, kw, sem, deps=()):
        fn = ("dma_start", kw)
        waits = self._waits(eng, deps)
        c = self.dcnt.get(id(sem), 0) + 16
        self.dcnt[id(sem)] = c
        self.ops[eng].append((fn, waits, (sem, 16)))
        t = Tok(sem, c)
        self.pending.append(t)
        return t

    def wait(self, eng, deps):
        waits = self._waits(eng, deps)
        if waits:
            self.ops[eng].append((None, waits, None))

    def barrier(self):
        toks = [self.last[e] for e in ("pe", "act", "dve", "pool")] + self.pending
        self.pending = []
        for e in self.ENG:
            self.wait(e, toks)
        return toks

    def emit(self, block):
        def run(name):
            def body(e):
                for fn, waits, inc in self.ops[name]:
                    for (s, v) in waits:
                        e.wait_ge(s, v)
                    if fn is None:
                        continue
                    ins = getattr(e, fn[0])(**fn[1])
                    if inc is not None:
                        ins.then_inc(inc[0], inc[1])
            return body
        block.tensor(run("pe"))
        block.scalar(run("act"))
        block.vector(run("dve"))
        block.gpsimd(run("pool"))
        block.sync(run("sp"))


class Ctx:
    pass


def rview(region, off, dt, shape):
    esz = 4 if dt == F32 else 2
    n = int(np.prod(shape[1:])) * esz
    v = region[:, off:off + n].bitcast(dt)
    if len(shape) == 3:
        v = v.rearrange("p (a b) -> p a b", a=shape[1])
    return v


def setup(nc, st):
    C = Ctx()
    C.nc = nc
    C.st = st
    C.S = Sched(nc, st)
    sb = lambda n, sh, dt: st.enter_context(nc.sbuf_tensor("sb_" + n, sh, dt))
    C.sb = sb
    C.RB = sb("RB", [128, 16 * NTP * 4], U8)
    C.RH = sb("RH", [128, 16 * NTP * 2], U8)
    C.RZ = sb("RZ", [128, 44 * NTP * 2], U8)
    C.Bf = rview(C.RB, 0, F32, [128, 16, NTP])
    C.Hb = rview(C.RH, 0, BF16, [128, 16, NTP])
    C.Zb = rview(C.RZ, 0, BF16, [128, 44, NTP])
    C.PS = [st.enter_context(nc.psum_tensor(f"ps{i}", [128, 512], F32)) for i in range(8)]
    C.psfree = [None] * 8
    C.identf = sb("identf", [128, 128], F32)
    C.identb = sb("identb", [128, 128], BF16)
    C.onesb = sb("onesb", [128, 128], BF16)
    C.gcols = sb("gcols", [128, NG], F32)
    C.tmp = [sb(f"tmp{i}", [128, NTP], F32) for i in range(3)]
    C.sq = [sb(f"sq{i}", [128, NTP], BF16) for i in range(2)]
    C.small = sb("small", [128, 64], F32)
    C.s_w = [C.S.new_sem(f"wslot{i}") for i in range(3)]
    C.s_ld = C.S.new_sem("ld")
    C.s_st = C.S.new_sem("st")
    C.s_x = [C.S.new_sem(f"xs{i}") for i in range(2)]
    C.s_stg = [C.S.new_sem(f"stg{i}") for i in range(4)]
    C.ldpool = {"sp": [C.S.new_sem(f"ldp{i}") for i in range(8)], "pool": [C.S.new_sem(f"ldq{i}") for i in range(6)]}
    C.ldi = {"sp": 0, "pool": 0}
    C.wfree = [None] * 3
    C.widx = 0
    C.mainbank = 0
    C.evq = 0
    return C


def ACT(C, out, in_, func, deps, **kw):
    return C.S.op("act", "activation", dict(out=out, in_=in_, func=func, **kw), deps)


def TT(C, out, in0, in1, op, deps, eng="dve"):
    return C.S.op(eng, "tensor_tensor", dict(out=out, in0=in0, in1=in1, op=op), deps)


def TS(C, out, in0, s1, s2, op0, op1, deps, eng="dve"):
    return C.S.op(eng, "tensor_scalar", dict(out=out, in0=in0, scalar1=s1, scalar2=s2, op0=op0, op1=op1), deps)


def TS1(C, out, in_, s, op, deps, eng="dve"):
    return C.S.op(eng, "tensor_single_scalar", dict(out=out, in_=in_, scalar=s, op=op), deps)


def STT(C, out, in0, scalar, in1, op0, op1, deps, eng="dve"):
    return C.S.op(eng, "scalar_tensor_tensor", dict(out=out, in0=in0, scalar=scalar, in1=in1, op0=op0, op1=op1), deps)


def RCP(C, out, in_, deps):
    return C.S.op("dve", "reciprocal", dict(out=out, in_=in_), deps)


def RMAX(C, out, in_, deps):
    return C.S.op("dve", "reduce_max", dict(out=out, in_=in_, axis=AX.X), deps)


def MM(C, out, lhsT, rhs, start, stop, deps=(), sig=False):
    return C.S.op("pe", "matmul", dict(out=out, lhsT=lhsT, rhs=rhs, start=start, stop=stop), deps, sig=sig)


def TR(C, out, in_, ident, deps=(), sig=False):
    return C.S.op("pe", "transpose", dict(out=out, in_=in_, identity=ident), deps, sig=sig)


def CP(C, eng, out, in_, deps):
    if eng == "act":
        return ACT(C, out, in_, AF.Copy, deps)
    return C.S.op(eng, "tensor_copy", dict(out=out, in_=in_), deps)


def DMA(C, eng, out, in_, sem, deps=()):
    return C.S.dma(eng, dict(out=out, in_=in_), sem, deps)


def newld(C, q="sp"):
    C.ldi[q] += 1
    return C.ldpool[q][C.ldi[q] % len(C.ldpool[q])]


def gcol(C, idx):
    return C.gcols[:, idx:idx + 1]


def gi(kind, layer, c):
    return (kind * 2 + layer) * 16 + c


def next_bank(C, pool=(0, 1, 2, 3, 4, 5)):
    b = pool[C.mainbank % len(pool)]
    C.mainbank += 1
    return b


def evac_engine(C):
    C.evq += 1
    return "act" if C.evq % 2 else "dve"


def ring_views(region, off, nslots, slot_bytes, kc, ncols):
    return [rview(region, off + i * slot_bytes, BF16, [128, kc, ncols]) for i in range(nslots)]


def mm_fm(C, W, kc, coltiles, ring, rhs_fn, evac_fn, deps, tb=TB):
    S = C.S
    nslots = len(ring)
    for (c0, n) in coltiles:
        slot = C.widx % nslots
        C.widx += 1
        wt = ring[slot]
        src = W[:, c0:c0 + n].rearrange("(k p) n -> p k n", p=128)
        ld = DMA(C, "pool", wt[:, :, 0:n], src, C.s_w[slot], deps=[C.wfree[slot]] + list(deps))
        last = None
        for mi in range(n // 128):
            for tbi, (t0, tn) in enumerate(tb):
                b = next_bank(C)
                ps = C.PS[b]
                for k in range(kc):
                    d = [ld, C.psfree[b]] + list(deps) if k == 0 else ()
                    last = MM(C, ps[:, 0:tn], wt[:, k, mi * 128:(mi + 1) * 128], rhs_fn(k, t0, tn),
                              k == 0, k == kc - 1, d, sig=(k == kc - 1))
                C.psfree[b] = evac_fn(c0 + mi * 128, tbi, ps[:, 0:tn], last)
        C.wfree[slot] = last


def sumsq_rstd(C, src_fn, nch, denom, out_rstd, deps):
    sqtok = [None, None]
    mmtok = [None, None]
    lastmm = None
    for c in range(nch):
        i = c % 2
        sqt = C.sq[i]
        sqtok[i] = ACT(C, sqt[:, 0:NT], src_fn(c), AF.Square, list(deps) + [mmtok[i]])
        for tbi, (t0, tn) in enumerate(TB):
            d = [sqtok[i]] + ([C.psfree[tbi]] if c == 0 else [])
            lastmm = MM(C, C.PS[tbi][:, 0:tn], C.onesb[:], sqt[:, t0:t0 + tn], c == 0, c == nch - 1, d, sig=(tbi == 2))
        mmtok[i] = lastmm
    return rstd_finalize(C, lastmm, denom, out_rstd)


def rstd_finalize(C, lastmm, denom, out_rstd):
    toks = []
    for tbi, (t0, tn) in enumerate(TB):
        a = TS(C, out_rstd[:, t0:t0 + tn], C.PS[tbi][:, 0:tn], 1.0 / denom, EPS, ALU.mult, ALU.add, [lastmm])
        C.psfree[tbi] = a
        toks.append(a)
    b = ACT(C, out_rstd[:, 0:NT], out_rstd[:, 0:NT], AF.Sqrt, toks)
    return RCP(C, out_rstd[:, 0:NT], out_rstd[:, 0:NT], [b])


def norm_to_H(C, kind, layer, deps):
    rstd = C.tmp[0]
    t = sumsq_rstd(C, lambda c: C.Bf[:, c, 0:NT], 16, float(D), rstd, deps)
    toks = []
    for c in range(16):
        toks.append(STT(C, C.Hb[:, c, 0:NT], C.Bf[:, c, 0:NT], gcol(C, gi(kind, layer, c)), rstd[:, 0:NT],
                        ALU.mult, ALU.mult, [t] + list(deps)))
    return toks


def post_norm_residual(C, kind, layer, xsrc_fn, deps, next_norm=None, store_to=None):
    rstd = C.tmp[0]
    t = sumsq_rstd(C, lambda c: C.Bf[:, c, 0:NT], 16, float(D), rstd, deps)
    xt = [rview(C.RZ, i * NTP * 4, F32, [128, NTP]) for i in range(2)]
    free = [None, None]
    toks = []
    sqtok = [None, None]
    mmtok = [None, None]
    lastmm = None
    for c in range(16):
        i = c % 2
        ld = DMA(C, "sp", xt[i][:, 0:NT], xsrc_fn(c), C.s_x[i], deps=[free[i]] + list(deps))
        a = STT(C, C.Bf[:, c, 0:NT], C.Bf[:, c, 0:NT], gcol(C, gi(kind, layer, c)), rstd[:, 0:NT], ALU.mult, ALU.mult, [t])
        b = TT(C, C.Bf[:, c, 0:NT], C.Bf[:, c, 0:NT], xt[i][:, 0:NT], ALU.add, [a, ld])
        free[i] = b
        toks.append(b)
        if store_to is not None:
            DMA(C, "sp", store_to[c], C.Bf[:, c, 0:NT], C.s_st, deps=[b])
        if next_norm is not None:
            sqtok[i] = ACT(C, C.sq[i][:, 0:NT], C.Bf[:, c, 0:NT], AF.Square, [b, mmtok[i]])
            for tbi, (t0, tn) in enumerate(TB):
                d = [sqtok[i]] + ([C.psfree[tbi]] if c == 0 else [])
                lastmm = MM(C, C.PS[tbi][:, 0:tn], C.onesb[:], C.sq[i][:, t0:t0 + tn], c == 0, c == 15, d, sig=(tbi == 2))
            mmtok[i] = lastmm
    if next_norm is not None:
        rstd2 = C.tmp[1]
        r2 = rstd_finalize(C, lastmm, float(D), rstd2)
        for c in range(16):
            toks.append(STT(C, C.Hb[:, c, 0:NT], C.Bf[:, c, 0:NT], gcol(C, gi(next_norm[0], next_norm[1], c)), rstd2[:, 0:NT],
                            ALU.mult, ALU.mult, [r2]))
    return toks


def store_B(C, dst, deps):
    return [DMA(C, "sp", dst[c], C.Bf[:, c, 0:NT], C.s_st, deps) for c in range(16)]


def ffn(C, layer, w_up, w_down, x1_scr, prenormed=False, next_norm=None):
    S = C.S
    if not prenormed:
        norm_to_H(C, 2, layer, [])
    S.barrier()
    ring = ring_views(C.RB, 0, 3, 8192, 16, 256)
    sg = [rview(C.RB, 24576 + i * NTP * 2, BF16, [128, NTP]) for i in range(4)]
    sgfree = [None] * 4
    sgtok = {}
    coltiles = []
    for t in range(22):
        coltiles.append((256 * t, 256))
        coltiles.append((DFF + 256 * t, 256))

    def evac_up(col0, tbi, ps, tok):
        t0, tn = TB[tbi]
        if col0 < DFF:
            j = col0 // 128
            i = j % 4
            r = ACT(C, sg[i][:, t0:t0 + tn], ps, AF.Silu, [tok, sgfree[i]])
            sgtok[(j, tbi)] = r
            return r
        j = (col0 - DFF) // 128
        i = j % 4
        r = TT(C, C.Zb[:, j, t0:t0 + tn], sg[i][:, t0:t0 + tn], ps, ALU.mult, [tok, sgtok[(j, tbi)]])
        if tbi == 2:
            sgfree[i] = r
        return r

    mm_fm(C, w_up, 16, coltiles, ring, lambda k, t0, tn: C.Hb[:, k, t0:t0 + tn], evac_up, [])
    S.barrier()
    ring = ring_views(C.RH, 0, 2, 11264, FC, 128)

    def evac_dn(col0, tbi, ps, tok):
        t0, tn = TB[tbi]
        return CP(C, evac_engine(C), C.Bf[:, col0 // 128, t0:t0 + tn], ps, [tok])

    mm_fm(C, w_down, FC, [(128 * m, 128) for m in range(16)], ring, lambda k, t0, tn: C.Zb[:, k, t0:t0 + tn], evac_dn, [])
    S.barrier()
    post_norm_residual(C, 3, layer, lambda c: x1_scr[c], [], next_norm=next_norm)
    S.barrier()


def mem_attention(C, qbase, KmT, Vm, KsT, Vs, deps):
    ex0, ex1 = C.tmp[2], C.tmp[0]
    cts = [rview(C.RB, h * NTP * 4, F32, [128, NTP]) for h in range(4)]
    kmx = C.small
    prev = list(deps)
    ctoks_all, cs_all = [], []
    for h in range(4):
        q = C.Zb[:, qbase + h, :]
        ct = cts[h]
        kt = []
        for si, KT in enumerate((KmT, KsT)):
            a = ACT(C, C.sq[0][:, 0:256], KT[:, h, :], AF.Square, prev)
            m = MM(C, C.PS[3][:, 0:256], C.onesb[:], C.sq[0][:, 0:256], True, True, [a, C.psfree[3]], sig=True)
            r = RMAX(C, kmx[:, si * 4 + h:si * 4 + h + 1], C.PS[3][:, 0:256], [m])
            C.psfree[3] = r
            prev = prev + [m]
            kt.append(r)
        a = ACT(C, C.sq[1][:, 0:NT], q[:, 0:NT], AF.Square, prev)
        ctoks = []
        for tbi, (t0, tn) in enumerate(TB):
            m = MM(C, C.PS[tbi][:, 0:tn], C.onesb[:], C.sq[1][:, t0:t0 + tn], True, True, [a, C.psfree[tbi]], sig=True)
            if tbi == 2:
                cs = TS(C, ct[:, 1025:1026], C.PS[2][:, 340:341], kmx[:, 4 + h:5 + h], 0.5 * SCALE, ALU.add, ALU.mult, [m, kt[1]])
            r = TS(C, ct[:, t0:t0 + tn], C.PS[tbi][:, 0:tn], kmx[:, h:h + 1], 0.5 * SCALE, ALU.add, ALU.mult, [m, kt[0]])
            C.psfree[tbi] = r
            ctoks.append(r)
            prev = prev + [m]
        ctoks_all.append(ctoks)
        cs_all.append(cs)
    for h in range(4):
        q = C.Zb[:, qbase + h, :]
        ct, ctoks, cs = cts[h], ctoks_all[h], cs_all[h]
        SB = (3, 4, 5)
        sbi = 0
        lastpe = None
        for tbi, (t0, tn) in enumerate(TB):
            pts = []
            for mb in range(2):
                bk = SB[sbi % 3]
                sbi += 1
                m = MM(C, C.PS[bk][:, 0:tn], KmT[:, h, mb * 128:(mb + 1) * 128], q[:, t0:t0 + tn], True, True,
                       [C.psfree[bk]] + prev, sig=True)
                exv = ex0 if mb == 0 else ex1
                a = STT(C, exv[:, t0:t0 + tn], C.PS[bk][:, 0:tn], SCALE, ct[:, t0:t0 + tn], ALU.mult, ALU.subtract, [m, ctoks[tbi]])
                C.psfree[bk] = a
                pt = C.sq[mb]
                p = ACT(C, pt[:, t0:t0 + tn], exv[:, t0:t0 + tn], AF.Exp, [a, lastpe] + prev)
                pts.append((pt, p))
            for mb in range(2):
                pt, p = pts[mb]
                MM(C, C.PS[6][:, 0:tn], Vm[:, mb, h * 128:(h + 1) * 128], pt[:, t0:t0 + tn], mb == 0, mb == 1,
                   [pts[0][1], pts[1][1], C.psfree[6]])
            om = None
            for mb in range(2):
                pt, p = pts[mb]
                om = MM(C, C.PS[7][:, 0:tn], C.onesb[:], pt[:, t0:t0 + tn], mb == 0, mb == 1, [C.psfree[7]], sig=(mb == 1))
            r = RCP(C, ex0[:, t0:t0 + tn], C.PS[7][:, 0:tn], [om])
            C.psfree[7] = r
            w = TT(C, C.Hb[:, 12 + h, t0:t0 + tn], C.PS[6][:, 0:tn], ex0[:, t0:t0 + tn], ALU.mult, [r])
            C.psfree[6] = w
            lastpe = om
        m = None
        for mb in range(2):
            m = MM(C, C.PS[3][:, mb:mb + 1], KsT[:, h, mb * 128:(mb + 1) * 128], q[:, 1024:1025], True, True,
                   [C.psfree[3]] + prev, sig=(mb == 1))
        a = TS(C, ex1[:, 0:2], C.PS[3][:, 0:2], SCALE, ct[:, 1025:1026], ALU.mult, ALU.subtract, [m, cs, w])
        C.psfree[3] = a
        p = ACT(C, C.sq[0][:, 0:2], ex1[:, 0:2], AF.Exp, [a, lastpe])
        for mb in range(2):
            MM(C, C.PS[6][:, 0:1], Vs[:, mb, h * 128:(h + 1) * 128], C.sq[0][:, mb:mb + 1], mb == 0, mb == 1, [p, C.psfree[6]])
        om = None
        for mb in range(2):
            om = MM(C, C.PS[7][:, 0:1], C.onesb[:], C.sq[0][:, mb:mb + 1], mb == 0, mb == 1, [p, C.psfree[7]], sig=(mb == 1))
        r = RCP(C, ex1[:, 0:1], C.PS[7][:, 0:1], [om])
        C.psfree[7] = r
        w = TT(C, C.Hb[:, 12 + h, 1024:1025], C.PS[6][:, 0:1], ex1[:, 0:1], ALU.mult, [r])
        C.psfree[6] = w
        prev = prev + [om, w]
    return prev


def load_consts(C, dr):
    s1, s2 = newld(C, "sp"), newld(C, "pool")
    t = [DMA(C, "sp", C.identf[:], dr["identf"], s1), DMA(C, "sp", C.gcols[:], dr["gpack"], s1),
         DMA(C, "pool", C.identb[:], dr["identf"], s2), DMA(C, "pool", C.onesb[:], dr["onesf"], s2)]
    return t


MKSTOP = [99]


def memkv_compute(C, layer, dr, out_kvT, KmT, Vm, soff, deps):
    memT = rview(C.RB, soff, F32, [128, 16, 256])
    memn = rview(C.RB, soff + 16384, BF16, [128, 16, 256])
    stg = [rview(C.RB, soff + 24576 + i * 1024, F32, [128, 256]) for i in range(2)]
    vT = rview(C.RB, soff + 26624, BF16, [128, 4, 256])
    ring = ring_views(C.RB, soff + 28672, 2, 8192, 16, 256)
    ld = DMA(C, "sp", memT, dr["memT"].rearrange("c p t -> p c t"), newld(C), deps=deps)
    sqtok = [None, None]
    mmt = [None, None]
    mm = None
    for c in range(16):
        i = c % 2
        sqtok[i] = ACT(C, C.sq[i][:, 0:256], memT[:, c, :], AF.Square, [ld, mmt[i]] + list(deps))
        mm = MM(C, C.PS[3][:, 0:256], C.onesb[:], C.sq[i][:, 0:256], c == 0, c == 15,
                [sqtok[i]] + ([C.psfree[3]] if c == 0 else []), sig=True)
        mmt[i] = mm
    if MKSTOP[0] == 1:
        return [mm]
    rs = C.tmp[1]
    a = TS(C, rs[:, 0:256], C.PS[3][:, 0:256], 1.0 / D, EPS, ALU.mult, ALU.add, [mm])
    C.psfree[3] = a
    b = ACT(C, rs[:, 0:256], rs[:, 0:256], AF.Sqrt, [a])
    r = RCP(C, rs[:, 0:256], rs[:, 0:256], [b])
    nt = [STT(C, memn[:, c, :], memT[:, c, :], gcol(C, gi(4, layer, c)), rs[:, 0:256], ALU.mult, ALU.mult, [r]) for c in range(16)]
    if MKSTOP[0] == 2:
        return nt
    W = dr["w_mem_kv"][layer]
    stfree = [None, None]
    outs = []
    last = None
    for ti in range(4):
        slot = ti % 2
        wt = ring[slot]
        src = W[:, ti * 256:(ti + 1) * 256].rearrange("(k p) n -> p k n", p=128)
        wl = DMA(C, "pool", wt, src, C.s_w[slot], deps=[C.wfree[slot]] + list(deps))
        for mi in range(2):
            m = ti * 2 + mi
            b_ = next_bank(C)
            for k in range(16):
                last = MM(C, C.PS[b_][:, 0:256], wt[:, k, mi * 128:(mi + 1) * 128], memn[:, k, :], k == 0, k == 15,
                          ([wl, C.psfree[b_]] + nt) if k == 0 else (), sig=(k == 15))
            si = m % 2
            dst = KmT[:, m, :] if m < 4 else vT[:, m - 4, :]
            e2 = CP(C, "dve", dst, C.PS[b_][:, 0:256], [last])
            if MKSTOP[0] == 31:
                C.psfree[b_] = [e2]
                outs += [e2]
                continue
            e1 = ACT(C, stg[si], C.PS[b_][:, 0:256], AF.Copy, [last, stfree[si], e2])
            C.psfree[b_] = [e1, e2]
            if MKSTOP[0] == 32:
                outs += [e1, e2]
                continue
            stfree[si] = DMA(C, "sp", out_kvT[m], stg[si], C.s_stg[si], deps=[e1])
            outs += [stfree[si], e2]
        C.wfree[slot] = last
    if MKSTOP[0] in (3, 31, 32):
        return outs
    psb = C.PS[6][:, :].bitcast(BF16)
    tl = None
    for h in range(4):
        for mb in range(2):
            first = (h == 0 and mb == 0)
            tl = TR(C, psb[:, (mb * 4 + h) * 128:(mb * 4 + h + 1) * 128], vT[:, h, mb * 128:(mb + 1) * 128], C.identb[:],
                    (outs + [C.psfree[6]]) if first else (), sig=(h == 3 and mb == 1))
    cp = CP(C, "dve", Vm, psb[:, 0:1024].rearrange("p (a b) -> p a b", a=2), [tl])
    C.psfree[6] = cp
    return outs + [cp]


class StopHere(Exception):
    pass


STOP = [99]


def stage(C, n):
    if STOP[0] == n:
        C.S.barrier()
        raise StopHere()


def layer0(C, dr, xsrc, memkv_mode="compute", next_norm=None):
    S = C.S
    sx = newld(C)
    lds = [DMA(C, "sp", C.Bf[:, c, 0:NT], xsrc[c], sx) for c in range(16)]
    norm_to_H(C, 0, 0, lds)
    S.barrier()
    stage(C, 1)
    KmT = rview(C.RB, 57344, BF16, [128, 4, 256])
    Vm = rview(C.RB, 59392, BF16, [128, 2, 512])
    KsT = rview(C.RB, 61440, BF16, [128, 4, 256])
    Vs = rview(C.RB, 63488, BF16, [128, 2, 512])
    bsb = rview(C.RB, 45056, F32, [128, 12, 128])
    mkv = rview(C.RB, 57344, BF16, [128, 2048])
    if memkv_mode == "load":
        mk = [DMA(C, "sp", mkv, dr["mkscr"], newld(C))]
    else:
        mk = memkv_compute(C, 0, dr, dr["o_memkvT"], KmT, Vm, 0, [])
        if "mkscr" in dr:
            DMA(C, "sp", dr["mkscr"], mkv, C.s_st, deps=list(mk))
    s1, s2 = newld(C, "pool"), newld(C, "sp")
    mk.append(DMA(C, "pool", KsT, dr["cmemKT"][0].rearrange("h p m -> p h m"), s1))
    mk.append(DMA(C, "pool", Vs, dr["cmemV"][0].rearrange("(b p) c -> p b c", p=128), s1))
    for j in range(12):
        mk.append(DMA(C, "sp", bsb[:, j, :], dr["b_s"][j // 3].partition_broadcast(128), s2))
    S.barrier()
    stage(C, 2)
    ring = ring_views(C.RB, 0, 3, 8192, 16, 256)
    gt = [rview(C.RB, 24576 + i * NTP * 4, F32, [128, NTP]) for i in range(4)]
    gfree = [None] * 4
    vsamp = C.small[:, 16:28]
    cnt = [0]

    def evac_in(col0, tbi, ps, tok):
        t0, tn = TB[tbi]
        m = col0 // 128
        if m >= 24:
            return CP(C, "act", C.Zb[:, m, t0:t0 + tn], ps, [tok])
        i = cnt[0] % 4
        cnt[0] += 1
        g = gt[i][:, 0:tn]
        a = ACT(C, g, ps, AF.Square, [tok, gfree[i]])
        b = TS(C, g, g, 0.044715, 1.0, ALU.mult, ALU.add, [a])
        c_ = TT(C, g, g, ps, ALU.mult, [b])
        d = ACT(C, g, g, AF.Sigmoid, [c_], scale=GELU_C)
        r = TT(C, C.Zb[:, m, t0:t0 + tn], g, ps, ALU.mult, [d])
        if 12 <= m < 24 and tbi == 2:
            r = TT(C, vsamp[:, m - 12:m - 11], gt[i][:, 340:341], ps[:, 340:341], ALU.mult, [d, r])
        gfree[i] = r
        return r

    mm_fm(C, dr["w_in_a"], 16, [(256 * t, 256) for t in range(14)], ring, lambda k, t0, tn: C.Hb[:, k, t0:t0 + tn], evac_in, [])
    S.barrier()
    stage(C, 3)
    rstdv = C.tmp[0]
    t = sumsq_rstd(C, lambda c: C.Zb[:, 12 + c, 0:NT], 12, float(MIXW), rstdv, [])
    vt = [STT(C, C.Zb[:, 12 + j, 0:NT], C.Zb[:, 12 + j, 0:NT], gcol(C, 160 + j), rstdv[:, 0:NT], ALU.mult, ALU.mult, [t])
          for j in range(12)]
    vso = C.small[:, 28:40]
    s1 = TT(C, vso, vsamp, C.gcols[:, 160:172], ALU.mult, [t])
    s2 = TS1(C, vso, vso, rstdv[:, 1024:1025], ALU.mult, [s1])
    DMA(C, "sp", dr["o_vrows"], vso, C.s_st, deps=[s2])
    vtm = rview(C.RZ, 28 * NTP * 2, BF16, [128, 9, MIXW])
    pa = C.PS[6][:, :].bitcast(BF16)
    pb = C.PS[7][:, :].bitcast(BF16)
    for ti in range(9):
        t0 = ti * 128
        tn = 128 if ti < 8 else 1
        la = lb = None
        for j in range(12):
            pp, jj, bk = (pa, j, 6) if j < 8 else (pb, j - 8, 7)
            x = TR(C, pp[0:tn, jj * 128:(jj + 1) * 128], C.Zb[:, 12 + j, t0:t0 + tn], C.identb[:],
                   (vt + [C.psfree[bk]]) if j in (0, 8) else (), sig=(j in (7, 11)))
            if j == 7:
                la = x
            if j == 11:
                lb = x
        C.psfree[6] = CP(C, "act", vtm[0:tn, ti, 0:1024], pa[0:tn, 0:1024], [la])
        C.psfree[7] = CP(C, "dve", vtm[0:tn, ti, 1024:1536], pb[0:tn, 0:512], [lb])
    vready = [C.psfree[6], C.psfree[7]]
    stage(C, 4)
    WmT = C.WmT
    g0 = rview(C.RB, 24576, F32, [128, 12, 128])
    g0free = None
    for ti in range(8):
        last = None
        for j in range(12):
            bk = j // 4
            last = MM(C, C.PS[bk][:, (j % 4) * 128:(j % 4 + 1) * 128], vtm[:, ti, j * 128:(j + 1) * 128], WmT[:, j // 3, :],
                      True, True, (vready + [C.psfree[bk]] + C.wm_ready) if j % 4 == 0 else (), sig=(j % 4 == 3))
            if j % 4 == 3:
                q = j // 4
                a = TT(C, g0[:, q * 4:(q + 1) * 4, :], C.PS[bk][:, :].rearrange("p (a b) -> p a b", a=4),
                       bsb[:, q * 4:(q + 1) * 4, :], ALU.add, [last, g0free] + mk)
                C.psfree[bk] = a
                b = TT(C, C.Hb[:, q * 4:(q + 1) * 4, ti * 128:(ti + 1) * 128], g0[:, q * 4:(q + 1) * 4, :],
                       C.Zb[:, q * 4:(q + 1) * 4, ti * 128:(ti + 1) * 128], ALU.mult, [a])
                if q == 2:
                    g0free = b
    last = None
    for j in range(12):
        last = MM(C, C.PS[3][:, j:j + 1], vtm[0:1, 8, j * 128:(j + 1) * 128], WmT[0:1, j // 3, 0:1], True, True,
                  (vready + [C.psfree[3]] + C.wm_ready) if j == 0 else (), sig=(j == 11))
    a = TT(C, g0[:, :, 0], C.PS[3][:, 0:12], bsb[:, :, 0], ALU.add, [last, g0free] + mk)
    C.psfree[3] = a
    sp_last = TT(C, C.Hb[:, 0:12, 1024], g0[:, :, 0], C.Zb[:, 0:12, 1024], ALU.mult, [a])
    stage(C, 5)
    mem_attention(C, 24, KmT, Vm, KsT, Vs, mk + [sp_last])
    S.barrier()
    stage(C, 6)
    ring = ring_views(C.RZ, 2 * NTP * 4, 3, 8192, 16, 256)

    def evac_o(col0, tbi, ps, tok):
        t0, tn = TB[tbi]
        return CP(C, evac_engine(C), C.Bf[:, col0 // 128, t0:t0 + tn], ps, [tok])

    mm_fm(C, dr["w_out"][0], 16, [(256 * t, 256) for t in range(8)], ring, lambda k, t0, tn: C.Hb[:, k, t0:t0 + tn], evac_o, [])
    S.barrier()
    stage(C, 7)
    post_norm_residual(C, 1, 0, lambda c: xsrc[c], [], next_norm=(2, 0), store_to=dr["x1scr"])
    S.barrier()
    stage(C, 8)
    ffn(C, 0, dr["w_ffn_up"][0], dr["w_ffn_down"][0], dr["x1scr"], prenormed=True, next_norm=next_norm)


def spatial_setup(C, dr):
    C.WmT = C.sb("WmT", [128, 4, 128], BF16)
    wtmp = rview(C.RZ, 0, F32, [128, 4, 128])
    mtmp = rview(C.RZ, 2048, F32, [128, 4, 128])
    s1 = newld(C)
    a = DMA(C, "sp", wtmp, dr["wsT"], s1)
    b = DMA(C, "sp", mtmp, dr["trilT"], s1)
    C.wm_ready = [TT(C, C.WmT[:], wtmp, mtmp, ALU.mult, [a, b])]


def dram_in(nc, name, shape):
    return nc.dram_tensor(name, list(shape), F32, kind="ExternalInput").ap()


def dram_out(nc, name, shape):
    return nc.dram_tensor(name, list(shape), F32, kind="ExternalOutput").ap()


def dump_debug(C, dr):
    S = C.S
    S.barrier()
    free = [None, None, None]
    n = 0
    for (src, cnt, base) in ((C.Zb, 44, 0), (C.Hb, 16, 44)):
        for c in range(cnt):
            i = n % 3
            n += 1
            a = CP(C, "dve", C.tmp[i][:, 0:NT], src[:, c, 0:NT], [free[i]])
            free[i] = DMA(C, "sp", dr["o_dbg"][base + c], C.tmp[i][:, 0:NT], C.s_stg[i], deps=[a])
    for c in range(16):
        DMA(C, "sp", dr["o_dbg"][60 + c], C.Bf[:, c, 0:NT], C.s_st)
    S.barrier()


def build_A():
    nc = bass.Bass("TRN2", target_bir_lowering=False)
    dr = {}
    if STOP[0] < 99:
        dr["o_dbg"] = dram_out(nc, "o_dbg", (76, 128, NT))
    for name, shape in [("xT", (16, 128, NT)), ("memT", (16, 128, 256)), ("cmemKT", (2, 4, 128, 256)), ("cmemV", (2, 256, 512)),
                        ("identf", (128, 128)), ("onesf", (128, 128)), ("gpack", (128, NG)), ("wsT", (128, 4, 128)),
                        ("trilT", (128, 4, 128)), ("b_s", (4, 128)), ("w_mem_kv", (2, D, 1024)), ("w_in_a", (D, 3584)),
                        ("w_out", (2, D, D)), ("w_ffn_up", (2, D, 2 * DFF)), ("w_ffn_down", (2, DFF, D)), ("w_kv_b", (D, 3072))]:
        dr[name] = dram_in(nc, name, shape)
    dr["o_memkvT"] = dram_out(nc, "o_memkvT", (8, 128, 256))
    dr["o_vrows"] = dram_out(nc, "o_vrows", (128, 12))
    dr["o_xmid"] = dram_out(nc, "o_xmid", (16, 128, NT))
    dr["o_kvT"] = dram_out(nc, "o_kvT", (24, 128, NT))
    dr["x1scr"] = nc.dram_tensor("x1scr", [16, 128, NT], F32).ap()
    with ExitStack() as st:
        C = setup(nc, st)
        S = C.S
        load_consts(C, dr)
        spatial_setup(C, dr)
        S.barrier()
        try:
            layer0(C, dr, dr["xT"])
        except StopHere:
            dump_debug(C, dr)
            with nc.Block() as block:
                S.emit(block)
            return nc
        store_B(C, dr["o_xmid"], [])
        norm_to_H(C, 0, 1, [])
        S.barrier()
        ring = ring_views(C.RZ, 0, 3, 8192, 16, 256)
        stg = [rview(C.RZ, 24576 + i * NTP * 4, F32, [128, NTP]) for i in range(4)]
        stfree = [None] * 4
        cnt = [0]

        def evac_kv(col0, tbi, ps, tok):
            t0, tn = TB[tbi]
            i = cnt[0] % 4
            cnt[0] += 1
            a = CP(C, evac_engine(C), stg[i][:, 0:tn], ps, [tok, stfree[i]])
            stfree[i] = DMA(C, "sp", dr["o_kvT"][col0 // 128][:, t0:t0 + tn], stg[i][:, 0:tn], C.s_stg[i], deps=[a])
            return a

        mm_fm(C, dr["w_kv_b"], 16, [(256 * t, 256) for t in range(12)], ring, lambda k, t0, tn: C.Hb[:, k, t0:t0 + tn], evac_kv, [])
        S.barrier()
        with nc.Block() as block:
            S.emit(block)
    return nc


def f32(a):
    return np.ascontiguousarray(np.asarray(a, dtype=np.float32))


def pack_gains(inp):
    rows = []
    for name in ("norm_mix_pre", "norm_mix_post", "norm_ffn_pre", "norm_ffn_post", "norm_mem"):
        g = np.asarray(inp[name], dtype=np.float32)
        for l in range(2):
            rows.append(g[l].reshape(16, 128))
    rows.append(np.asarray(inp["norm_v_a"], dtype=np.float32)[0].reshape(12, 128))
    return f32(np.concatenate(rows, axis=0).T)


def common_inputs(inp):
    ws = np.asarray(inp["w_spatial_a"], dtype=np.float32)[0]
    tril = np.tril(np.ones((128, 128), np.float32))
    return {
        "identf": np.eye(128, dtype=np.float32),
        "onesf": np.ones((128, 128), np.float32),
        "gpack": pack_gains(inp),
        "wsT": f32(ws.transpose(2, 0, 1)),
        "trilT": f32(np.broadcast_to(tril.T[:, None, :], (128, 4, 128))),
        "b_s": f32(np.asarray(inp["b_spatial_a"])[0]),
    }


def inputs_A(inp):
    xp = np.asarray(inp["x_prompt"], dtype=np.float32)
    xs = np.asarray(inp["x_sample"], dtype=np.float32)
    mem = np.asarray(inp["mem_prompt"], dtype=np.float32)
    cmem = np.asarray(inp["cache_mem_kv"], dtype=np.float32)
    w_in_b = np.asarray(inp["w_in_b"], dtype=np.float32)[0]
    shared = common_inputs(inp)
    shared.update({
        "w_mem_kv": f32(inp["w_mem_kv"]), "w_in_a": f32(np.asarray(inp["w_in_a"])[0]), "w_out": f32(inp["w_out"]),
        "w_ffn_up": f32(inp["w_ffn_up"]), "w_ffn_down": f32(inp["w_ffn_down"]),
        "w_kv_b": f32(w_in_b[:, 1536:4608]),
    })
    in_maps = []
    for c in range(8):
        b, p = c // 4, c % 4
        xtok = np.concatenate([xp[b, p * 1024:(p + 1) * 1024], xs[c]], axis=0)
        m = dict(shared)
        m["xT"] = f32(xtok.T.reshape(16, 128, NT))
        m["memT"] = f32(mem[b].T.reshape(16, 128, 256))
        m["cmemKT"] = f32(cmem[:, c, :, 0].transpose(0, 2, 3, 1))
        m["cmemV"] = f32(cmem[:, c, :, 1].reshape(2, 256, 512))
        in_maps.append(m)
    return in_maps


def run_A(inp):
    nc = build_A()
    res = run_bass_kernel_spmd(nc, inputs_A(inp), core_ids=list(range(8)))
    return res.results


GROUPS = ((128, 1), (512, 4), (2048, 16))


def tok_slice(r, b, bs, d, nblk=1):
    start = b * bs * d + r
    n = bs * nblk
    return slice(start, start + (n - 1) * d + 1, d)


def bias_setup(C, dr):
    rb = rview(C.RZ, 0, F32, [128, 12])
    Sg = rview(C.RZ, 64, F32, [128, 3, 384])
    vm = rview(C.RZ, 64 + 4608, F32, [128, 384])
    gt = rview(C.RZ, 64 + 4608 + 1536, F32, [128, 3, 384])
    s1 = newld(C)
    l1 = DMA(C, "sp", rb[0:32, :], dr["rel_bias"], s1)
    l2 = DMA(C, "sp", Sg[0:32, :, :], dr["Soh"].rearrange("g b m -> b g m"), s1)
    l3 = DMA(C, "sp", vm[0:4, :], dr["vmask"], s1)
    l1 = l2 = l3 = [l1, l2, l3]
    outs = []
    for g in range(3):
        m = MM(C, C.PS[g][0:4, 0:384], rb[0:32, 4 * g:4 * g + 4], Sg[0:32, g, :], True, True, [l1, l2, C.psfree[g]], sig=True)
        a = ACT(C, gt[0:4, g, :], C.PS[g][0:4, 0:384], AF.Exp, [m])
        C.psfree[g] = a
        b = TT(C, gt[0:4, g, :], gt[0:4, g, :], vm[0:4, :], ALU.mult, [a, l3])
        outs.append(DMA(C, "sp", dr["Gscr"][4 * g:4 * g + 4, :], gt[0:4, g, :], C.s_st, deps=[b]))
    C.Jf = C.sb("Jf", [128, 128], F32)
    outs.append(DMA(C, "sp", C.Jf[:], dr["Jf"], newld(C)))
    return outs


def attention_l1(C, dr, deps):
    S = C.S
    RB = C.RB
    off = [0]

    def alloc(dt, shape):
        v = rview(RB, off[0], dt, shape)
        esz = 4 if dt == F32 else 2
        off[0] += int(np.prod(shape[1:])) * esz
        return v

    oun = [alloc(F32, [128, NTP]) for _ in range(3)]
    den = [alloc(F32, [128, NTP]) for _ in range(3)]
    cst = [alloc(F32, [128, NTP]) for _ in range(3)]
    Et = [alloc(F32, [128, 256]) for _ in range(3)]
    Ep1 = alloc(F32, [128, 256])
    Ep = [Ep1, Ep1, Ep1]
    vtm = [alloc(BF16, [128, 9, 128]), alloc(BF16, [128, 8, 128]), alloc(BF16, [128, 16, 128])]
    vself = alloc(BF16, [128, 3, 128])
    hkn = [alloc(BF16, [128, 128]), alloc(BF16, [128, 512]), alloc(BF16, [128, 2048])]
    ckt = [alloc(BF16, [128, 128]) for _ in range(3)]
    hv = [alloc(BF16, [128, 2, 128]), alloc(BF16, [128, 5, 128]), alloc(BF16, [128, 17, 128])]
    hvn = C.tmp[2][:, :].bitcast(BF16)
    assert off[0] <= 16 * NTP * 4, off[0]
    hvc = C.small[:, 40:44]
    kmx = C.small[:, 44:48]
    ldv = DMA(C, "sp", hvc, dr["hvalid"], newld(C))
    t0_, t1_, t2_ = C.tmp
    prev = list(deps) + [ldv]
    SBK = (0, 1, 2, 3)
    sbi = [0]
    pbf = C.PS[3][:, :].bitcast(BF16)
    for h in range(4):
        lds = []
        sp_, se_ = newld(C, "pool"), newld(C, "sp")
        for g, (win, d) in enumerate(GROUPS):
            hh = 4 * g + h
            nh = 128 * d
            lds.append(DMA(C, "pool", hkn[g], dr["KVh"][hh][:, 2048 - nh:2048], sp_, deps=prev))
            lds.append(DMA(C, "pool", ckt[g], dr[f"cK{g}"][h], sp_, deps=prev))
            lds.append(DMA(C, "pool", hv[g][:, d, :], dr[f"cV{g}"][:, h * 128:(h + 1) * 128], sp_, deps=prev))
        lds = [lds]
        hvfree = prev
        epfree = None
        for g, (win, d) in enumerate(GROUPS):
            hh = 4 * g + h
            nh = 128 * d
            lv = DMA(C, "pool", hvn[:, 0:nh], dr["KVh"][12 + hh][:, 2048 - nh:2048], newld(C, "pool"), deps=hvfree)
            x = None
            for i0 in range(0, d, 8):
                n = min(8, d - i0)
                for i in range(n):
                    r = i0 + i
                    x = TR(C, pbf[:, i * 128:(i + 1) * 128], hvn[:, r:r + 127 * d + 1:d], C.identb[:],
                           ([lv, C.psfree[3]] + prev) if i == 0 else (), sig=(i == n - 1))
                e = CP(C, "act", hv[g][:, i0:i0 + n, :], pbf[:, 0:n * 128].rearrange("p (a b) -> p a b", a=n), [x] + prev)
                C.psfree[3] = e
                lds.append(e)
            hvfree = [x]
            src = bass.AP(dr["Gscr"].tensor, hh * 384, [[1, 128], [1, 256]])
            bk = SBK[sbi[0] % 3]
            sbi[0] += 1
            le = DMA(C, "sp", Ep[g], src, se_, deps=prev + C.gs_ready + [epfree])
            m = MM(C, C.PS[bk][:, 0:256], C.Jf[:], Ep[g], True, True, [le, C.psfree[bk]], sig=True)
            epfree = m
            e = CP(C, "dve", Et[g], C.PS[bk][:, 0:256], [m] + prev)
            C.psfree[bk] = e
            lds.append(e)
        readys = []
        for g, (win, d) in enumerate(GROUPS):
            hh = 4 * g + h
            R = d
            L = 1024 // d
            bs = min(128, L)
            nb = L // bs
            qT = C.Zb[:, hh, :]
            kT = C.Zb[:, 12 + hh, :]
            vT = C.Zb[:, 24 + hh, :]
            nhk = (R + 1) * 128
            mx = []
            a = ACT(C, C.sq[0][:, 0:NT], kT[:, 0:NT], AF.Square, prev)
            for tbi, (t0, tn) in enumerate(TB):
                m = MM(C, C.PS[tbi][:, 0:tn], C.onesb[:], C.sq[0][:, t0:t0 + tn], True, True, [a, C.psfree[tbi]], sig=True)
                r = RMAX(C, C.small[:, 48 + len(mx):49 + len(mx)], C.PS[tbi][:, 0:tn], [m])
                C.psfree[tbi] = r
                mx.append(r)
                prev = prev + [m]
            ksrcs = [hkn[g][:, c0:c0 + min(512, R * 128 - c0)] for c0 in range(0, R * 128, 512)] + [ckt[g][:, :]]
            for ksrc in ksrcs:
                cn = ksrc.shape[1]
                a = ACT(C, C.sq[1][:, 0:cn], ksrc, AF.Square, prev + lds)
                bk = SBK[sbi[0] % 3]
                sbi[0] += 1
                m = MM(C, C.PS[bk][:, 0:cn], C.onesb[:], C.sq[1][:, 0:cn], True, True, [a, C.psfree[bk]], sig=True)
                r = RMAX(C, C.small[:, 48 + len(mx):49 + len(mx)], C.PS[bk][:, 0:cn], [m])
                C.psfree[bk] = r
                mx.append(r)
                prev = prev + [m]
            km = RMAX(C, kmx[:, g:g + 1], C.small[:, 48:48 + len(mx)], mx)
            a = ACT(C, C.sq[0][:, 0:NT], qT[:, 0:NT], AF.Square, prev)
            ctoks = []
            for tbi, (t0, tn) in enumerate(TB):
                m = MM(C, C.PS[tbi][:, 0:tn], C.onesb[:], C.sq[0][:, t0:t0 + tn], True, True, [a, C.psfree[tbi]], sig=True)
                r = TS(C, cst[g][:, t0:t0 + tn], C.PS[tbi][:, 0:tn], kmx[:, g:g + 1], 0.5 * SCALE, ALU.add, ALU.mult, [m, km])
                C.psfree[tbi] = r
                ctoks.append(r)
                prev = prev + [m]
            vt_toks = []
            blocks = [(r, b) for r in range(R) for b in range(nb)]
            if g == 0:
                blocks = [(0, b) for b in range(8)]
            for i0 in range(0, len(blocks), 8):
                grp = blocks[i0:i0 + 8]
                x = None
                for i, (r, b) in enumerate(grp):
                    x = TR(C, pbf[0:bs, i * 128:(i + 1) * 128], vT[:, tok_slice(r, b, bs, d)], C.identb[:],
                           (prev + [C.psfree[3]]) if i == 0 else (), sig=(i == len(grp) - 1))
                e = CP(C, "act", vtm[g][0:bs, i0:i0 + len(grp), :], pbf[0:bs, 0:len(grp) * 128].rearrange("p (a b) -> p a b", a=len(grp)), [x])
                C.psfree[3] = e
                vt_toks.append(e)
            x = TR(C, pbf[0:1, 0:128], vT[:, 1024:1025], C.identb[:], prev + [C.psfree[3]], sig=True)
            e = CP(C, "act", vself[0:1, g, :], pbf[0:1, 0:128], [x])
            C.psfree[3] = e
            vt_toks.append(e)
            readys.append(prev + lds + ctoks + vt_toks)
        for g, (win, d) in enumerate(GROUPS):
            hh = 4 * g + h
            R = d
            L = 1024 // d
            bs = min(128, L)
            nb = L // bs
            qT = C.Zb[:, hh, :]
            kT = C.Zb[:, 12 + hh, :]
            ready = readys[g] + prev
            units = []
            for r in range(R + 1):
                sample = (r == R)
                rbs = 1 if sample else bs
                rnb = 1 if sample else nb
                for kb in range(-1, rnb):
                    u = dict(r=r, kb=kb, sample=sample, rbs=rbs)
                    if kb < 0:
                        u["K"] = ckt[g][:, :] if sample else hkn[g][:, r:r + 127 * d + 1:d]
                        u["nk"] = 128
                        u["V"] = hv[g][:, r, :]
                        u["qsl"] = slice(1024, 1025) if sample else tok_slice(r, 0, bs, d)
                        u["nq"] = rbs
                        u["qbs"] = [0]
                    else:
                        if sample:
                            u["K"] = kT[:, 1024:1025]
                            u["V"] = vself[0:1, g, :]
                            u["qsl"] = slice(1024, 1025)
                        else:
                            u["K"] = kT[:, tok_slice(r, kb, bs, d)]
                            u["V"] = vtm[g][0:bs, (r * nb + kb) if g else kb, :]
                            u["qsl"] = tok_slice(r, kb, bs, d, 2 if kb + 1 < rnb else 1)
                        u["nk"] = rbs
                        u["qbs"] = [kb, kb + 1] if kb + 1 < rnb else [kb]
                        u["nq"] = rbs * len(u["qbs"])
                    units.append(u)
            n = len(units)
            tA = [t0_[:, 256 * j:256 * (j + 1)] for j in range(4)]
            tB = [t1_[:, 256 * j:256 * (j + 1)] for j in range(4)]
            PTs = [C.sq[j // 4][:, 256 * (j % 4):256 * (j % 4 + 1)] for j in range(8)]
            tAfree = [None] * 4
            tBfree = [None] * 4
            PTfree = [None] * 8
            atok = [None] * n
            etok = [None] * n
            lastpe = None
            mtok = [None] * n
            ptok = [None] * n
            pend = [None] * n
            SB4 = (0, 1)
            for i in range(n + 5):
                if i < n:
                    u = units[i]
                    bk = SB4[i % 2]
                    mtok[i] = MM(C, C.PS[bk][0:u["nk"], 0:u["nq"]], u["K"], qT[:, u["qsl"]], True, True,
                                 ready + [C.psfree[bk]], sig=True)
                j = i - 1
                if 0 <= j < n:
                    u = units[j]
                    nk, nq = u["nk"], u["nq"]
                    bk = SB4[j % 2]
                    atok[j] = STT(C, tA[j % 4][0:nk, 0:nq], C.PS[bk][0:nk, 0:nq], SCALE, cst[g][0:nk, u["qsl"]],
                                  ALU.mult, ALU.subtract, [mtok[j], tAfree[j % 4]])
                    C.psfree[bk] = atok[j]
                j = i - 2
                if 0 <= j < n:
                    u = units[j]
                    nk, nq = u["nk"], u["nq"]
                    etok[j] = ACT(C, tB[j % 4][0:nk, 0:nq], tA[j % 4][0:nk, 0:nq], AF.Exp, [atok[j], tBfree[j % 4]])
                    tAfree[j % 4] = etok[j]
                j = i - 3
                if 0 <= j < n:
                    u = units[j]
                    nk, nq, kb, qbs, sample = u["nk"], u["nq"], u["kb"], u["qbs"], u["sample"]
                    PT = PTs[j % 8]
                    e = etok[j]
                    if kb < 0:
                        vc = hvc[:, 2:3] if sample else (hvc[:, 1:2] if g == 2 else hvc[:, 0:1])
                        p = STT(C, PT[0:nk, 0:nq], tB[j % 4][0:nk, 0:nq], vc, Et[g][0:nk, 128:128 + nq], ALU.mult, ALU.mult,
                                [e, PTfree[j % 8]])
                    elif len(qbs) == 2:
                        p = TT(C, PT[0:nk, 0:nq], tB[j % 4][0:nk, 0:nq], Et[g][0:nk, 0:256], ALU.mult, [e, PTfree[j % 8]])
                    else:
                        p = TT(C, PT[0:nk, 0:nq], tB[j % 4][0:nk, 0:nq], Et[g][0:nk, 0:nq], ALU.mult, [e, PTfree[j % 8]])
                    tBfree[j % 4] = p
                    ptok[j] = p
                j = i - 5
                if 0 <= j < n and pend[j]:
                    u = units[j]
                    rbs, r, sample = u["rbs"], u["r"], u["sample"]
                    for (qb, ob, db, last) in pend[j]:
                        osl = slice(1024, 1025) if sample else tok_slice(r, qb, bs, d)
                        C.psfree[ob] = CP(C, "act", oun[g][:, osl], C.PS[ob][:, 0:rbs], [last])
                        C.psfree[db] = CP(C, "dve", den[g][:, osl], C.PS[db][:, 0:rbs], [last])
                j = i - 4
                if 0 <= j < n:
                    u = units[j]
                    nk, rbs, r, kb, qbs = u["nk"], u["rbs"], u["r"], u["kb"], u["qbs"]
                    PT = PTs[j % 8]
                    p = ptok[j]
                    last = None
                    fin = []
                    for qi, qb in enumerate(qbs):
                        first = (kb == qb - 1)
                        final = (kb == qb)
                        ob = 2 + ((qb + r) % 3)
                        db = 5 + ((qb + r) % 3)
                        cols = slice(qi * rbs, (qi + 1) * rbs)
                        MM(C, C.PS[ob][:, 0:rbs], u["V"], PT[0:nk, cols], first, final, [p, C.psfree[ob]] if first else [p])
                        last = MM(C, C.PS[db][:, 0:rbs], C.onesb[0:nk, :], PT[0:nk, cols], first, final,
                                  [p, C.psfree[db]] if first else [p], sig=True)
                        if final:
                            fin.append((qb, ob, db, last))
                    PTfree[j % 8] = last
                    lastpe = last
                    pend[j] = fin
            prev = prev + [lastpe] + [x for x in etok[-4:]] + [x for x in atok[-3:]]
            prev = prev + [C.psfree[b_] for b_ in range(2, 8)]
        a = TT(C, t0_[:, 0:NT], cst[0][:, 0:NT], cst[1][:, 0:NT], ALU.max, prev)
        a = TT(C, t0_[:, 0:NT], t0_[:, 0:NT], cst[2][:, 0:NT], ALU.max, [a])
        ws = []
        for g in range(3):
            b = TT(C, cst[g][:, 0:NT], cst[g][:, 0:NT], t0_[:, 0:NT], ALU.subtract, [a])
            b = ACT(C, cst[g][:, 0:NT], cst[g][:, 0:NT], AF.Exp, [b])
            b = TT(C, den[g][:, 0:NT], den[g][:, 0:NT], cst[g][:, 0:NT], ALU.mult, [b])
            ws.append(b)
        b = TT(C, t1_[:, 0:NT], den[0][:, 0:NT], den[1][:, 0:NT], ALU.add, ws)
        b = TT(C, t1_[:, 0:NT], t1_[:, 0:NT], den[2][:, 0:NT], ALU.add, [b])
        b = RCP(C, t1_[:, 0:NT], t1_[:, 0:NT], [b])
        outs = []
        for g in range(3):
            x = TT(C, cst[g][:, 0:NT], cst[g][:, 0:NT], t1_[:, 0:NT], ALU.mult, [b])
            outs.append(TT(C, C.Hb[:, 4 * g + h, 0:NT], oun[g][:, 0:NT], cst[g][:, 0:NT], ALU.mult, [x]))
        prev = prev + outs
    return prev


def layer1(C, dr, xsrc, memkv_key="o_memkvT", preloaded=False):
    S = C.S
    if not preloaded:
        sx = newld(C)
        lds = [DMA(C, "sp", C.Bf[:, c, 0:NT], xsrc[c], sx) for c in range(16)]
        norm_to_H(C, 0, 1, lds)
    S.barrier()
    zo = 40 * NTP * 2
    KmT = rview(C.RZ, zo, BF16, [128, 4, 256])
    Vm = rview(C.RZ, zo + 2048, BF16, [128, 2, 512])
    KsT = rview(C.RZ, zo + 4096, BF16, [128, 4, 256])
    Vs = rview(C.RZ, zo + 6144, BF16, [128, 2, 512])
    mk = memkv_compute(C, 1, dr, dr[memkv_key], KmT, Vm, 0, [])
    s1 = newld(C, "pool")
    mk.append(DMA(C, "pool", KsT, dr["cmemKT"][1].rearrange("h p m -> p h m"), s1))
    mk.append(DMA(C, "pool", Vs, dr["cmemV"][1].rearrange("(b p) c -> p b c", p=128), s1))
    S.barrier()
    ring = ring_views(C.RB, 0, 3, 8192, 16, 256)

    def evac_in(col0, tbi, ps, tok):
        t0, tn = TB[tbi]
        return CP(C, evac_engine(C), C.Zb[:, col0 // 128, t0:t0 + tn], ps, [tok])

    mm_fm(C, dr["w_in_b"], 16, [(256 * t, 256) for t in range(20)], ring, lambda k, t0, tn: C.Hb[:, k, t0:t0 + tn], evac_in, [])
    S.barrier()
    p = attention_l1(C, dr, [])
    mem_attention(C, 36, KmT, Vm, KsT, Vs, p)
    S.barrier()
    ring = ring_views(C.RZ, 2 * NTP * 4, 3, 8192, 16, 256)

    def evac_o(col0, tbi, ps, tok):
        t0, tn = TB[tbi]
        return CP(C, evac_engine(C), C.Bf[:, col0 // 128, t0:t0 + tn], ps, [tok])

    mm_fm(C, dr["w_out"][1], 16, [(256 * t, 256) for t in range(8)], ring, lambda k, t0, tn: C.Hb[:, k, t0:t0 + tn], evac_o, [])
    S.barrier()
    post_norm_residual(C, 1, 1, lambda c: xsrc[c], [], next_norm=(2, 1), store_to=dr["x1scr"])
    S.barrier()
    ffn(C, 1, dr["w_ffn_up"][1], dr["w_ffn_down"][1], dr["x1scr"], prenormed=True)


def build_B():
    nc = bass.Bass("TRN2", target_bir_lowering=False)
    dr = {}
    for name, shape in [("xT", (16, 128, NT)), ("memT", (16, 128, 256)), ("cmemKT", (2, 4, 128, 256)), ("cmemV", (2, 256, 512)),
                        ("identf", (128, 128)), ("onesf", (128, 128)), ("gpack", (128, NG)), ("Jf", (128, 128)),
                        ("rel_bias", (32, 12)), ("Soh", (3, 32, 384)), ("vmask", (4, 384)), ("hvalid", (128, 4)),
                        ("KVh", (24, 128, 2048)), ("cK0", (4, 128, 128)), ("cK1", (4, 128, 128)), ("cK2", (4, 128, 128)),
                        ("cV0", (128, 512)), ("cV1", (128, 512)), ("cV2", (128, 512)),
                        ("w_mem_kv", (2, D, 1024)), ("w_in_b", (D, 5120)),
                        ("w_out", (2, D, D)), ("w_ffn_up", (2, D, 2 * DFF)), ("w_ffn_down", (2, DFF, D))]:
        dr[name] = dram_in(nc, name, shape)
    dr["o_memkvT"] = dram_out(nc, "o_memkvT", (8, 128, 256))
    dr["o_y"] = dram_out(nc, "o_y", (16, 128, NT))
    dr["x1scr"] = nc.dram_tensor("x1scr", [16, 128, NT], F32).ap()
    dr["Gscr"] = nc.dram_tensor("Gscr", [12, 384], F32).ap()
    with ExitStack() as st:
        C = setup(nc, st)
        S = C.S
        load_consts(C, dr)
        C.gs_ready = bias_setup(C, dr)
        S.barrier()
        layer1(C, dr, dr["xT"])
        store_B(C, dr["o_y"], [])
        S.barrier()
        with nc.Block() as block:
            S.emit(block)
    return nc


def t5_bucket_np(dist):
    nf = np.maximum(dist, 16).astype(np.float32)
    large = 16 + (np.log(nf / 16) / np.log(2048 / 16) * 16).astype(np.int32)
    large = np.minimum(large, 31)
    return np.where(dist < 16, dist, large)


def bias_tables():
    Soh = np.zeros((3, 32, 384), np.float32)
    vmask = np.zeros((4, 384), np.float32)
    for g, (win, d) in enumerate(GROUPS):
        j = np.arange(129, dtype=np.int32)
        bk = t5_bucket_np(j * d)
        Soh[g, bk, j + 127] = 1.0
    vmask[:, 127:256] = 1.0
    return Soh, vmask


def cache_inputs(caches, c):
    m = {}
    i = np.arange(128)
    for g, (win, d) in enumerate(GROUPS):
        rows = caches[g][c][i * d]
        m[f"cK{g}"] = f32(rows[:, 0].transpose(1, 2, 0))
        m[f"cV{g}"] = f32(rows[:, 1].reshape(128, 512))
    return m


def inputs_B(inp, ra):
    mem = np.asarray(inp["mem_prompt"], dtype=np.float32)
    cmem = np.asarray(inp["cache_mem_kv"], dtype=np.float32)
    caches = [np.asarray(inp[n], dtype=np.float32)[0] for n in ("cache_win128_kv", "cache_win512_kv", "cache_win2048_kv")]
    Soh, vmask = bias_tables()
    shared = {
        "identf": np.eye(128, dtype=np.float32), "onesf": np.ones((128, 128), np.float32), "gpack": pack_gains(inp),
        "Jf": f32(np.eye(128, dtype=np.float32)[::-1]), "rel_bias": f32(inp["rel_bias"]), "Soh": Soh, "vmask": vmask,
        "w_mem_kv": f32(inp["w_mem_kv"]), "w_in_b": f32(np.asarray(inp["w_in_b"])[0]), "w_out": f32(inp["w_out"]),
        "w_ffn_up": f32(inp["w_ffn_up"]), "w_ffn_down": f32(inp["w_ffn_down"]),
    }
    KT = []
    VT = []
    for b in range(2):
        KT.append(np.concatenate([ra[4 * b + p]["o_kvT"][0:12, :, 0:1024] for p in range(4)], axis=2))
        VT.append(np.concatenate([ra[4 * b + p]["o_kvT"][12:24, :, 0:1024] for p in range(4)], axis=2))
    in_maps = []
    for c in range(8):
        b, p = c // 4, c % 4
        s = p * 1024
        m = dict(shared)
        m["xT"] = f32(ra[c]["o_xmid"])
        m["memT"] = f32(mem[b].T.reshape(16, 128, 256))
        m["cmemKT"] = f32(cmem[:, c, :, 0].transpose(0, 2, 3, 1))
        m["cmemV"] = f32(cmem[:, c, :, 1].reshape(2, 256, 512))
        hvalid = np.zeros((128, 4), np.float32)
        hvalid[:, 0] = 1.0 if p >= 1 else 0.0
        hvalid[:64, 1] = 1.0 if p >= 2 else 0.0
        hvalid[64:, 1] = 1.0 if p >= 1 else 0.0
        hvalid[:, 2] = 1.0
        m["hvalid"] = hvalid
        KVh = np.zeros((24, 128, 2048), np.float32)
        n = min(s, 2048)
        if n > 0:
            KVh[0:12, :, 2048 - n:] = KT[b][:, :, s - n:s]
            KVh[12:24, :, 2048 - n:] = VT[b][:, :, s - n:s]
        m["KVh"] = KVh
        m.update(cache_inputs(caches, c))
        in_maps.append(m)
    return in_maps


def run_B(inp, ra):
    nc = build_B()
    res = run_bass_kernel_spmd(nc, inputs_B(inp, ra), core_ids=list(range(8)))
    return res.results


def assemble(inp, ra, rb):
    y_prompt = np.zeros((2, 4096, D), np.float32)
    y_sample = np.zeros((8, 1, D), np.float32)
    memkv = np.zeros((2, 2, 256, 2, 4, 128), np.float32)
    vrows = np.zeros((1, 8, 1, MIXW), np.float32)
    winp = [np.zeros((1, 2, min(w, 4096), 2, 4, 128), np.float32) for (w, d) in GROUPS]
    wins = [np.zeros((1, 8, 1, 2, 4, 128), np.float32) for _ in GROUPS]
    for c in range(8):
        b, p = c // 4, c % 4
        y = rb[c]["o_y"].reshape(D, NT).T
        y_prompt[b, p * 1024:(p + 1) * 1024] = y[:1024]
        y_sample[c, 0] = y[1024]
        vrows[0, c, 0] = ra[c]["o_vrows"].T.reshape(MIXW)
        kvT = ra[c]["o_kvT"]
        for g in range(3):
            wins[g][0, c, 0, 0] = kvT[4 * g:4 * g + 4, :, 1024]
            wins[g][0, c, 0, 1] = kvT[12 + 4 * g:16 + 4 * g, :, 1024]
    for b in range(2):
        memkv[0, b] = ra[4 * b]["o_memkvT"].reshape(1024, 256).T.reshape(256, 2, 4, 128)
        memkv[1, b] = rb[4 * b]["o_memkvT"].reshape(1024, 256).T.reshape(256, 2, 4, 128)
        KT = np.concatenate([ra[4 * b + p]["o_kvT"][0:12, :, 0:1024] for p in range(4)], axis=2)
        VT = np.concatenate([ra[4 * b + p]["o_kvT"][12:24, :, 0:1024] for p in range(4)], axis=2)
        for g, (w, d) in enumerate(GROUPS):
            n = min(w, 4096)
            winp[g][0, b, :, 0] = KT[4 * g:4 * g + 4, :, 4096 - n:].transpose(2, 0, 1)
            winp[g][0, b, :, 1] = VT[4 * g:4 * g + 4, :, 4096 - n:].transpose(2, 0, 1)
    return (y_prompt, y_sample, memkv, vrows, winp[0], winp[1], winp[2], wins[0], wins[1], wins[2])


TBH = [(0, 342), (342, 342), (684, 340)]


def kv_proj(C, dr, coltiles, dst, tcol0, tb):
    ring = ring_views(C.RZ, 0, 3, 8192, 16, 256)
    stg = [rview(C.RZ, 24576 + i * NTP * 4, F32, [128, NTP]) for i in range(4)]
    stfree = [None] * 4
    cnt = [0]
    W = dr["w_in_b"][:, 1536:4608]

    def evac_kv(col0, tbi, ps, tok):
        t0, tn = tb[tbi]
        i = cnt[0] % 4
        cnt[0] += 1
        a = CP(C, evac_engine(C), stg[i][:, 0:tn], ps, [tok, stfree[i]])
        stfree[i] = DMA(C, "sp", dst[col0 // 128][:, tcol0 + t0:tcol0 + t0 + tn], stg[i][:, 0:tn], C.s_stg[i], deps=[a])
        return a

    mm_fm(C, W, 16, coltiles, ring, lambda k, t0, tn: C.Hb[:, k, t0:t0 + tn], evac_kv, [], tb=tb)


def build_F():
    nc = bass.Bass("TRN2", target_bir_lowering=False)
    dr = {}
    for name, shape in [("xT", (16, 128, NT)), ("xTh", (2, 16, 128, NT)), ("memT", (16, 128, 256)), ("cmemKT", (2, 4, 128, 256)),
                        ("cmemV", (2, 256, 512)), ("identf", (128, 128)), ("onesf", (128, 128)), ("gpack", (128, NG)),
                        ("wsT", (128, 4, 128)), ("trilT", (128, 4, 128)), ("b_s", (4, 128)), ("Jf", (128, 128)),
                        ("rel_bias", (32, 12)), ("Soh", (3, 32, 384)), ("vmask", (4, 384)), ("hvalid", (128, 4)),
                        ("cK0", (4, 128, 128)), ("cK1", (4, 128, 128)), ("cK2", (4, 128, 128)),
                        ("cV0", (128, 512)), ("cV1", (128, 512)), ("cV2", (128, 512)),
                        ("w_mem_kv", (2, D, 1024)), ("w_in_a", (D, 3584)), ("w_in_b", (D, 5120)),
                        ("w_out", (2, D, D)), ("w_ffn_up", (2, D, 2 * DFF)), ("w_ffn_down", (2, DFF, D))]:
        dr[name] = dram_in(nc, name, shape)
    dr["o_memkvT"] = dram_out(nc, "o_memkvT", (8, 128, 256))
    dr["o_memkvT1"] = dram_out(nc, "o_memkvT1", (8, 128, 256))
    dr["o_vrows"] = dram_out(nc, "o_vrows", (128, 12))
    dr["o_kvT"] = dram_out(nc, "o_kvT", (24, 128, NT))
    dr["o_y"] = dram_out(nc, "o_y", (16, 128, NT))
    dr["x1scr"] = nc.dram_tensor("x1scr", [16, 128, NT], F32).ap()
    dr["xmid"] = nc.dram_tensor("xmid", [16, 128, NT], F32).ap()
    dr["KVh"] = nc.dram_tensor("KVh", [24, 128, 2048], F32).ap()
    dr["Gscr"] = nc.dram_tensor("Gscr", [12, 384], F32).ap()
    dr["mkscr"] = nc.dram_tensor("mkscr", [128, 2048], BF16).ap()
    with ExitStack() as st:
        C = setup(nc, st)
        S = C.S
        load_consts(C, dr)
        spatial_setup(C, dr)
        S.barrier()
        C.gs_ready = bias_setup(C, dr)
        S.barrier()
        g2tiles = [(1024, 256), (1280, 256), (2560, 256), (2816, 256)]
        alltiles = [(256 * t, 256) for t in range(12)]
        for blk in range(2):
            layer0(C, dr, dr["xTh"][blk], memkv_mode=("compute" if blk == 0 else "load"), next_norm=(0, 1))
            kv_proj(C, dr, g2tiles if blk == 0 else alltiles, dr["KVh"], blk * 1024, TBH)
            S.barrier()
        layer0(C, dr, dr["xT"], memkv_mode="load", next_norm=(0, 1))
        store_B(C, dr["xmid"], [])
        S.barrier()
        kv_proj(C, dr, alltiles, dr["o_kvT"], 0, TB)
        S.barrier()
        layer1(C, dr, dr["xmid"], memkv_key="o_memkvT1", preloaded=True)
        store_B(C, dr["o_y"], [])
        S.barrier()
        with nc.Block() as block:
            S.emit(block)
    return nc


def inputs_F(inp):
    xp = np.asarray(inp["x_prompt"], dtype=np.float32)
    xs = np.asarray(inp["x_sample"], dtype=np.float32)
    mem = np.asarray(inp["mem_prompt"], dtype=np.float32)
    cmem = np.asarray(inp["cache_mem_kv"], dtype=np.float32)
    caches = [np.asarray(inp[n], dtype=np.float32)[0] for n in ("cache_win128_kv", "cache_win512_kv", "cache_win2048_kv")]
    Soh, vmask = bias_tables()
    shared = common_inputs(inp)
    shared.update({
        "Jf": f32(np.eye(128, dtype=np.float32)[::-1]), "rel_bias": f32(inp["rel_bias"]), "Soh": Soh, "vmask": vmask,
        "w_mem_kv": f32(inp["w_mem_kv"]), "w_in_a": f32(np.asarray(inp["w_in_a"])[0]), "w_in_b": f32(np.asarray(inp["w_in_b"])[0]),
        "w_out": f32(inp["w_out"]), "w_ffn_up": f32(inp["w_ffn_up"]), "w_ffn_down": f32(inp["w_ffn_down"]),
    })
    in_maps = []
    for c in range(8):
        b, p = c // 4, c % 4
        s = p * 1024
        m = dict(shared)
        xtok = np.concatenate([xp[b, s:s + 1024], xs[c]], axis=0)
        m["xT"] = f32(xtok.T.reshape(16, 128, NT))
        xh = np.zeros((2, 1025, D), np.float32)
        for blk in range(2):
            t0 = s - 2048 + blk * 1024
            if t0 >= 0:
                xh[blk, :1024] = xp[b, t0:t0 + 1024]
        m["xTh"] = f32(xh.transpose(0, 2, 1).reshape(2, 16, 128, NT))
        m["memT"] = f32(mem[b].T.reshape(16, 128, 256))
        m["cmemKT"] = f32(cmem[:, c, :, 0].transpose(0, 2, 3, 1))
        m["cmemV"] = f32(cmem[:, c, :, 1].reshape(2, 256, 512))
        hvalid = np.zeros((128, 4), np.float32)
        hvalid[:, 0] = 1.0 if p >= 1 else 0.0
        hvalid[:64, 1] = 1.0 if p >= 2 else 0.0
        hvalid[64:, 1] = 1.0 if p >= 1 else 0.0
        hvalid[:, 2] = 1.0
        m["hvalid"] = hvalid
        m.update(cache_inputs(caches, c))
        in_maps.append(m)
    return in_maps


def run_F(inp):
    nc = build_F()
    res = run_bass_kernel_spmd(nc, inputs_F(inp), core_ids=list(range(8)))
    return res.results


def kernel_two(**inp):
    ra = run_A(inp)
    rb = run_B(inp, ra)
    return assemble(inp, ra, rb)


def kernel(**inp):
    r = run_F(inp)
    rb = [{"o_y": x["o_y"], "o_memkvT": x["o_memkvT1"]} for x in r]
    return assemble(inp, r, rb)
```

```python
import numpy as np
from contextlib import ExitStack
import concourse.bass as bass
import concourse.mybir as mybir
from concourse.bass_utils import run_bass_kernel_spmd

F32 = mybir.dt.float32
BF16 = mybir.dt.bfloat16
U8 = mybir.dt.uint8
AF = mybir.ActivationFunctionType
ALU = mybir.AluOpType
AX = mybir.AxisListType

D = 2048
KC = 16
NPR = 1024
NT = 1025
NTP = 1032
TB = [(0, 342), (342, 342), (684, 341)]
MIXW = 1536
DFF = 5632
FC = 44
SCALE = float(128 ** -0.5)
EPS = 1e-6
NG = 172
GELU_C = 1.5957691216057308


class Tok:
    __slots__ = ("sem", "val")

    def __init__(self, sem, val):
        self.sem = sem
        self.val = val


class Sched:
    ENG = ("pe", "act", "dve", "pool", "sp")

    def __init__(self, nc, stack):
        self.nc = nc
        self.stack = stack
        self.ops = {e: [] for e in self.ENG}
        self.prog = {e: stack.enter_context(nc.semaphore("prog_" + e)) for e in self.ENG}
        self.cnt = {e: 0 for e in self.ENG}
        self.seen = {e: {} for e in self.ENG}
        self.last = {e: None for e in self.ENG}
        self.dcnt = {}
        self.pending = []
        self.nsem = 0

    def new_sem(self, name=None):
        self.nsem += 1
        return self.stack.enter_context(self.nc.semaphore(name or f"s{self.nsem}"))

    def _flat(self, deps, acc):
        for d in deps:
            if d is None:
                continue
            if isinstance(d, (list, tuple)):
                self._flat(d, acc)
            else:
                k = id(d.sem)
                if k not in acc or acc[k].val < d.val:
                    acc[k] = d
        return acc

    def _waits(self, eng, deps, out=None):
        out = []
        for k, d in self._flat(deps, {}).items():
            if self.seen[eng].get(k, -1) >= d.val:
                continue
            self.seen[eng][k] = d.val
            out.append((d.sem, d.val))
        return out

    def op(self, eng, name, kw, deps=(), sig=None):
        if sig is None:
            sig = eng != "pe"
        fn = (name, kw)
        waits = self._waits(eng, deps)
        tok = None
        if sig:
            self.cnt[eng] += 1
            tok = Tok(self.prog[eng], self.cnt[eng])
            self.last[eng] = tok
        self.ops[eng].append((fn, waits, (self.prog[eng], 1) if sig else None))
        return tok

    def dma(self, eng, kw, sem, deps=()):
        fn = ("dma_start", kw)
        waits = self._waits(eng, deps)
        c = self.dcnt.get(id(sem), 0) + 16
        self.dcnt[id(sem)] = c
        self.ops[eng].append((fn, waits, (sem, 16)))
        t = Tok(sem, c)
        self.pending.append(t)
        return t

    def wait(self, eng, deps):
        waits = self._waits(eng, deps)
        if waits:
            self.ops[eng].append((None, waits, None))

    def barrier(self):
        toks = [self.last[e] for e in ("pe", "act", "dve", "pool")] + self.pending
        self.pending = []
        for e in self.ENG:
            self.wait(e, toks)
        return toks

    def emit(self, block):
        def run(name):
            def body(e):
                for fn, waits, inc in self.ops[name]:
                    for (s, v) in waits:
                        e.wait_ge(s, v)
                    if fn is None:
                        continue
                    ins = getattr(e, fn[0])(**fn[1])
                    if inc is not None:
                        ins.then_inc(inc[0], inc[1])
            return body
        block.tensor(run("pe"))
        block.scalar(run("act"))
        block.vector(run("dve"))
        block.gpsimd(run("pool"))
        block.sync(run("sp"))


class Ctx:
    pass


def rview(region, off, dt, shape):
    esz = 4 if dt == F32 else 2
    n = int(np.prod(shape[1:])) * esz
    v = region[:, off:off + n].bitcast(dt)
    if len(shape) == 3:
        v = v.rearrange("p (a b) -> p a b", a=shape[1])
    return v


def setup(nc, st):
    C = Ctx()
    C.nc = nc
    C.st = st
    C.S = Sched(nc, st)
    sb = lambda n, sh, dt: st.enter_context(nc.sbuf_tensor("sb_" + n, sh, dt))
    C.sb = sb
    C.RB = sb("RB", [128, 16 * NTP * 4], U8)
    C.RH = sb("RH", [128, 16 * NTP * 2], U8)
    C.RZ = sb("RZ", [128, 44 * NTP * 2], U8)
    C.Bf = rview(C.RB, 0, F32, [128, 16, NTP])
    C.Hb = rview(C.RH, 0, BF16, [128, 16, NTP])
    C.Zb = rview(C.RZ, 0, BF16, [128, 44, NTP])
    C.PS = [st.enter_context(nc.psum_tensor(f"ps{i}", [128, 512], F32)) for i in range(8)]
    C.psfree = [None] * 8
    C.identf = sb("identf", [128, 128], F32)
    C.identb = sb("identb", [128, 128], BF16)
    C.onesb = sb("onesb", [128, 128], BF16)
    C.gcols = sb("gcols", [128, NG], F32)
    C.tmp = [sb(f"tmp{i}", [128, NTP], F32) for i in range(3)]
    C.sq = [sb(f"sq{i}", [128, NTP], BF16) for i in range(2)]
    C.small = sb("small", [128, 64], F32)
    C.s_w = [C.S.new_sem(f"wslot{i}") for i in range(3)]
    C.s_ld = C.S.new_sem("ld")
    C.s_st = C.S.new_sem("st")
    C.s_x = [C.S.new_sem(f"xs{i}") for i in range(2)]
    C.s_stg = [C.S.new_sem(f"stg{i}") for i in range(4)]
    C.ldpool = {"sp": [C.S.new_sem(f"ldp{i}") for i in range(8)], "pool": [C.S.new_sem(f"ldq{i}") for i in range(6)]}
    C.ldi = {"sp": 0, "pool": 0}
    C.wfree = [None] * 3
    C.widx = 0
    C.mainbank = 0
    C.evq = 0
    return C


def ACT(C, out, in_, func, deps, **kw):
    return C.S.op("act", "activation", dict(out=out, in_=in_, func=func, **kw), deps)


def TT(C, out, in0, in1, op, deps, eng="dve"):
    return C.S.op(eng, "tensor_tensor", dict(out=out, in0=in0, in1=in1, op=op), deps)


def TS(C, out, in0, s1, s2, op0, op1, deps, eng="dve"):
    return C.S.op(eng, "tensor_scalar", dict(out=out, in0=in0, scalar1=s1, scalar2=s2, op0=op0, op1=op1), deps)


def TS1(C, out, in_, s, op, deps, eng="dve"):
    return C.S.op(eng, "tensor_single_scalar", dict(out=out, in_=in_, scalar=s, op=op), deps)


def STT(C, out, in0, scalar, in1, op0, op1, deps, eng="dve"):
    return C.S.op(eng, "scalar_tensor_tensor", dict(out=out, in0=in0, scalar=scalar, in1=in1, op0=op0, op1=op1), deps)


def RCP(C, out, in_, deps):
    return C.S.op("dve", "reciprocal", dict(out=out, in_=in_), deps)


def RMAX(C, out, in_, deps):
    return C.S.op("dve", "reduce_max", dict(out=out, in_=in_, axis=AX.X), deps)


def MM(C, out, lhsT, rhs, start, stop, deps=(), sig=False):
    return C.S.op("pe", "matmul", dict(out=out, lhsT=lhsT, rhs=rhs, start=start, stop=stop), deps, sig=sig)


def TR(C, out, in_, ident, deps=(), sig=False):
    return C.S.op("pe", "transpose", dict(out=out, in_=in_, identity=ident), deps, sig=sig)


def CP(C, eng, out, in_, deps):
    if eng == "act":
        return ACT(C, out, in_, AF.Copy, deps)
    return C.S.op(eng, "tensor_copy", dict(out=out, in_=in_), deps)


def DMA(C, eng, out, in_, sem, deps=()):
    return C.S.dma(eng, dict(out=out, in_=in_), sem, deps)


def newld(C, q="sp"):
    C.ldi[q] += 1
    return C.ldpool[q][C.ldi[q] % len(C.ldpool[q])]


def gcol(C, idx):
    return C.gcols[:, idx:idx + 1]


def gi(kind, layer, c):
    return (kind * 2 + layer) * 16 + c


def next_bank(C, pool=(0, 1, 2, 3, 4, 5)):
    b = pool[C.mainbank % len(pool)]
    C.mainbank += 1
    return b


def evac_engine(C):
    C.evq += 1
    return "act" if C.evq % 2 else "dve"


def ring_views(region, off, nslots, slot_bytes, kc, ncols):
    return [rview(region, off + i * slot_bytes, BF16, [128, kc, ncols]) for i in range(nslots)]


def mm_fm(C, W, kc, coltiles, ring, rhs_fn, evac_fn, deps, tb=TB):
    S = C.S
    nslots = len(ring)
    for (c0, n) in coltiles:
        slot = C.widx % nslots
        C.widx += 1
        wt = ring[slot]
        src = W[:, c0:c0 + n].rearrange("(k p) n -> p k n", p=128)
        ld = DMA(C, "pool", wt[:, :, 0:n], src, C.s_w[slot], deps=[C.wfree[slot]] + list(deps))
        last = None
        for mi in range(n // 128):
            for tbi, (t0, tn) in enumerate(tb):
                b = next_bank(C)
                ps = C.PS[b]
                for k in range(kc):
                    d = [ld, C.psfree[b]] + list(deps) if k == 0 else ()
                    last = MM(C, ps[:, 0:tn], wt[:, k, mi * 128:(mi + 1) * 128], rhs_fn(k, t0, tn),
                              k == 0, k == kc - 1, d, sig=(k == kc - 1))
                C.psfree[b] = evac_fn(c0 + mi * 128, tbi, ps[:, 0:tn], last)
        C.wfree[slot] = last


def sumsq_rstd(C, src_fn, nch, denom, out_rstd, deps):
    sqtok = [None, None]
    mmtok = [None, None]
    lastmm = None
    for c in range(nch):
        i = c % 2
        sqt = C.sq[i]
        sqtok[i] = ACT(C, sqt[:, 0:NT], src_fn(c), AF.Square, list(deps) + [mmtok[i]])
        for tbi, (t0, tn) in enumerate(TB):
            d = [sqtok[i]] + ([C.psfree[tbi]] if c == 0 else [])
            lastmm = MM(C, C.PS[tbi][:, 0:tn], C.onesb[:], sqt[:, t0:t0 + tn], c == 0, c == nch - 1, d, sig=(tbi == 2))
        mmtok[i] = lastmm
    return rstd_finalize(C, lastmm, denom, out_rstd)


def rstd_finalize(C, lastmm, denom, out_rstd):
    toks = []
    for tbi, (t0, tn) in enumerate(TB):
        a = TS(C, out_rstd[:, t0:t0 + tn], C.PS[tbi][:, 0:tn], 1.0 / denom, EPS, ALU.mult, ALU.add, [lastmm])
        C.psfree[tbi] = a
        toks.append(a)
    b = ACT(C, out_rstd[:, 0:NT], out_rstd[:, 0:NT], AF.Sqrt, toks)
    return RCP(C, out_rstd[:, 0:NT], out_rstd[:, 0:NT], [b])


def norm_to_H(C, kind, layer, deps):
    rstd = C.tmp[0]
    t = sumsq_rstd(C, lambda c: C.Bf[:, c, 0:NT], 16, float(D), rstd, deps)
    toks = []
    for c in range(16):
        toks.append(STT(C, C.Hb[:, c, 0:NT], C.Bf[:, c, 0:NT], gcol(C, gi(kind, layer, c)), rstd[:, 0:NT],
                        ALU.mult, ALU.mult, [t] + list(deps)))
    return toks


def post_norm_residual(C, kind, layer, xsrc_fn, deps, next_norm=None, store_to=None):
    rstd = C.tmp[0]
    t = sumsq_rstd(C, lambda c: C.Bf[:, c, 0:NT], 16, float(D), rstd, deps)
    xt = [rview(C.RZ, i * NTP * 4, F32, [128, NTP]) for i in range(2)]
    free = [None, None]
    toks = []
    sqtok = [None, None]
    mmtok = [None, None]
    lastmm = None
    for c in range(16):
        i = c % 2
        ld = DMA(C, "sp", xt[i][:, 0:NT], xsrc_fn(c), C.s_x[i], deps=[free[i]] + list(deps))
        a = STT(C, C.Bf[:, c, 0:NT], C.Bf[:, c, 0:NT], gcol(C, gi(kind, layer, c)), rstd[:, 0:NT], ALU.mult, ALU.mult, [t])
        b = TT(C, C.Bf[:, c, 0:NT], C.Bf[:, c, 0:NT], xt[i][:, 0:NT], ALU.add, [a, ld])
        free[i] = b
        toks.append(b)
        if store_to is not None:
            DMA(C, "sp", store_to[c], C.Bf[:, c, 0:NT], C.s_st, deps=[b])
        if next_norm is not None:
            sqtok[i] = ACT(C, C.sq[i][:, 0:NT], C.Bf[:, c, 0:NT], AF.Square, [b, mmtok[i]])
            for tbi, (t0, tn) in enumerate(TB):
                d = [sqtok[i]] + ([C.psfree[tbi]] if c == 0 else [])
                lastmm = MM(C, C.PS[tbi][:, 0:tn], C.onesb[:], C.sq[i][:, t0:t0 + tn], c == 0, c == 15, d, sig=(tbi == 2))
            mmtok[i] = lastmm
    if next_norm is not None:
        rstd2 = C.tmp[1]
        r2 = rstd_finalize(C, lastmm, float(D), rstd2)
        for c in range(16):
            toks.append(STT(C, C.Hb[:, c, 0:NT], C.Bf[:, c, 0:NT], gcol(C, gi(next_norm[0], next_norm[1], c)), rstd2[:, 0:NT],
                            ALU.mult, ALU.mult, [r2]))
    return toks


def store_B(C, dst, deps):
    return [DMA(C, "sp", dst[c], C.Bf[:, c, 0:NT], C.s_st, deps) for c in range(16)]


def ffn(C, layer, w_up, w_down, x1_scr, prenormed=False, next_norm=None):
    S = C.S
    if not prenormed:
        norm_to_H(C, 2, layer, [])
    S.barrier()
    ring = ring_views(C.RB, 0, 3, 8192, 16, 256)
    sg = [rview(C.RB, 24576 + i * NTP * 2, BF16, [128, NTP]) for i in range(4)]
    sgfree = [None] * 4
    sgtok = {}
    coltiles = []
    for t in range(22):
        coltiles.append((256 * t, 256))
        coltiles.append((DFF + 256 * t, 256))

    def evac_up(col0, tbi, ps, tok):
        t0, tn = TB[tbi]
        if col0 < DFF:
            j = col0 // 128
            i = j % 4
            r = ACT(C, sg[i][:, t0:t0 + tn], ps, AF.Silu, [tok, sgfree[i]])
            sgtok[(j, tbi)] = r
            return r
        j = (col0 - DFF) // 128
        i = j % 4
        r = TT(C, C.Zb[:, j, t0:t0 + tn], sg[i][:, t0:t0 + tn], ps, ALU.mult, [tok, sgtok[(j, tbi)]])
        if tbi == 2:
            sgfree[i] = r
        return r

    mm_fm(C, w_up, 16, coltiles, ring, lambda k, t0, tn: C.Hb[:, k, t0:t0 + tn], evac_up, [])
    S.barrier()
    ring = ring_views(C.RH, 0, 2, 11264, FC, 128)

    def evac_dn(col0, tbi, ps, tok):
        t0, tn = TB[tbi]
        return CP(C, evac_engine(C), C.Bf[:, col0 // 128, t0:t0 + tn], ps, [tok])

    mm_fm(C, w_down, FC, [(128 * m, 128) for m in range(16)], ring, lambda k, t0, tn: C.Zb[:, k, t0:t0 + tn], evac_dn, [])
    S.barrier()
    post_norm_residual(C, 3, layer, lambda c: x1_scr[c], [], next_norm=next_norm)
    S.barrier()


def mem_attention(C, qbase, KmT, Vm, KsT, Vs, deps):
    ex0, ex1 = C.tmp[2], C.tmp[0]
    cts = [rview(C.RB, h * NTP * 4, F32, [128, NTP]) for h in range(4)]
    kmx = C.small
    prev = list(deps)
    ctoks_all, cs_all = [], []
    for h in range(4):
        q = C.Zb[:, qbase + h, :]
        ct = cts[h]
        kt = []
        for si, KT in enumerate((KmT, KsT)):
            a = ACT(C, C.sq[0][:, 0:256], KT[:, h, :], AF.Square, prev)
            m = MM(C, C.PS[3][:, 0:256], C.onesb[:], C.sq[0][:, 0:256], True, True, [a, C.psfree[3]], sig=True)
            r = RMAX(C, kmx[:, si * 4 + h:si * 4 + h + 1], C.PS[3][:, 0:256], [m])
            C.psfree[3] = r
            prev = prev + [m]
            kt.append(r)
        a = ACT(C, C.sq[1][:, 0:NT], q[:, 0:NT], AF.Square, prev)
        ctoks = []
        for tbi, (t0, tn) in enumerate(TB):
            m = MM(C, C.PS[tbi][:, 0:tn], C.onesb[:], C.sq[1][:, t0:t0 + tn], True, True, [a, C.psfree[tbi]], sig=True)
            if tbi == 2:
                cs = TS(C, ct[:, 1025:1026], C.PS[2][:, 340:341], kmx[:, 4 + h:5 + h], 0.5 * SCALE, ALU.add, ALU.mult, [m, kt[1]])
            r = TS(C, ct[:, t0:t0 + tn], C.PS[tbi][:, 0:tn], kmx[:, h:h + 1], 0.5 * SCALE, ALU.add, ALU.mult, [m, kt[0]])
            C.psfree[tbi] = r
            ctoks.append(r)
            prev = prev + [m]
        ctoks_all.append(ctoks)
        cs_all.append(cs)
    for h in range(4):
        q = C.Zb[:, qbase + h, :]
        ct, ctoks, cs = cts[h], ctoks_all[h], cs_all[h]
        SB = (3, 4, 5)
        sbi = 0
        lastpe = None
        for tbi, (t0, tn) in enumerate(TB):
            pts = []
            for mb in range(2):
                bk = SB[sbi % 3]
                sbi += 1
                m = MM(C, C.PS[bk][:, 0:tn], KmT[:, h, mb * 128:(mb + 1) * 128], q[:, t0:t0 + tn], True, True,
                       [C.psfree[bk]] + prev, sig=True)
                exv = ex0 if mb == 0 else ex1
                a = STT(C, exv[:, t0:t0 + tn], C.PS[bk][:, 0:tn], SCALE, ct[:, t0:t0 + tn], ALU.mult, ALU.subtract, [m, ctoks[tbi]])
                C.psfree[bk] = a
                pt = C.sq[mb]
                p = ACT(C, pt[:, t0:t0 + tn], exv[:, t0:t0 + tn], AF.Exp, [a, lastpe] + prev)
                pts.append((pt, p))
            for mb in range(2):
                pt, p = pts[mb]
                MM(C, C.PS[6][:, 0:tn], Vm[:, mb, h * 128:(h + 1) * 128], pt[:, t0:t0 + tn], mb == 0, mb == 1,
                   [pts[0][1], pts[1][1], C.psfree[6]])
            om = None
            for mb in range(2):
                pt, p = pts[mb]
                om = MM(C, C.PS[7][:, 0:tn], C.onesb[:], pt[:, t0:t0 + tn], mb == 0, mb == 1, [C.psfree[7]], sig=(mb == 1))
            r = RCP(C, ex0[:, t0:t0 + tn], C.PS[7][:, 0:tn], [om])
            C.psfree[7] = r
            w = TT(C, C.Hb[:, 12 + h, t0:t0 + tn], C.PS[6][:, 0:tn], ex0[:, t0:t0 + tn], ALU.mult, [r])
            C.psfree[6] = w
            lastpe = om
        m = None
        for mb in range(2):
            m = MM(C, C.PS[3][:, mb:mb + 1], KsT[:, h, mb * 128:(mb + 1) * 128], q[:, 1024:1025], True, True,
                   [C.psfree[3]] + prev, sig=(mb == 1))
        a = TS(C, ex1[:, 0:2], C.PS[3][:, 0:2], SCALE, ct[:, 1025:1026], ALU.mult, ALU.subtract, [m, cs, w])
        C.psfree[3] = a
        p = ACT(C, C.sq[0][:, 0:2], ex1[:, 0:2], AF.Exp, [a, lastpe])
        for mb in range(2):
            MM(C, C.PS[6][:, 0:1], Vs[:, mb, h * 128:(h + 1) * 128], C.sq[0][:, mb:mb + 1], mb == 0, mb == 1, [p, C.psfree[6]])
        om = None
        for mb in range(2):
            om = MM(C, C.PS[7][:, 0:1], C.onesb[:], C.sq[0][:, mb:mb + 1], mb == 0, mb == 1, [p, C.psfree[7]], sig=(mb == 1))
        r = RCP(C, ex1[:, 0:1], C.PS[7][:, 0:1], [om])
        C.psfree[7] = r
        w = TT(C, C.Hb[:, 12 + h, 1024:1025], C.PS[6][:, 0:1], ex1[:, 0:1], ALU.mult, [r])
        C.psfree[6] = w
        prev = prev + [om, w]
    return prev


def load_consts(C, dr):
    s1, s2 = newld(C, "sp"), newld(C, "pool")
    t = [DMA(C, "sp", C.identf[:], dr["identf"], s1), DMA(C, "sp", C.gcols[:], dr["gpack"], s1),
         DMA(C, "pool", C.identb[:], dr["identf"], s2), DMA(C, "pool", C.onesb[:], dr["onesf"], s2)]
    return t


MKSTOP = [99]


def memkv_compute(C, layer, dr, out_kvT, KmT, Vm, soff, deps):
    memT = rview(C.RB, soff, F32, [128, 16, 256])
    memn = rview(C.RB, soff + 16384, BF16, [128, 16, 256])
    stg = [rview(C.RB, soff + 24576 + i * 1024, F32, [128, 256]) for i in range(2)]
    vT = rview(C.RB, soff + 26624, BF16, [128, 4, 256])
    ring = ring_views(C.RB, soff + 28672, 2, 8192, 16, 256)
    ld = DMA(C, "sp", memT, dr["memT"].rearrange("c p t -> p c t"), newld(C), deps=deps)
    sqtok = [None, None]
    mmt = [None, None]
    mm = None
    for c in range(16):
        i = c % 2
        sqtok[i] = ACT(C, C.sq[i][:, 0:256], memT[:, c, :], AF.Square, [ld, mmt[i]] + list(deps))
        mm = MM(C, C.PS[3][:, 0:256], C.onesb[:], C.sq[i][:, 0:256], c == 0, c == 15,
                [sqtok[i]] + ([C.psfree[3]] if c == 0 else []), sig=True)
        mmt[i] = mm
    if MKSTOP[0] == 1:
        return [mm]
    rs = C.tmp[1]
    a = TS(C, rs[:, 0:256], C.PS[3][:, 0:256], 1.0 / D, EPS, ALU.mult, ALU.add, [mm])
    C.psfree[3] = a
    b = ACT(C, rs[:, 0:256], rs[:, 0:256], AF.Sqrt, [a])
    r = RCP(C, rs[:, 0:256], rs[:, 0:256], [b])
    nt = [STT(C, memn[:, c, :], memT[:, c, :], gcol(C, gi(4, layer, c)), rs[:, 0:256], ALU.mult, ALU.mult, [r]) for c in range(16)]
    if MKSTOP[0] == 2:
        return nt
    W = dr["w_mem_kv"][layer]
    stfree = [None, None]
    outs = []
    last = None
    for ti in range(4):
        slot = ti % 2
        wt = ring[slot]
        src = W[:, ti * 256:(ti + 1) * 256].rearrange("(k p) n -> p k n", p=128)
        wl = DMA(C, "pool", wt, src, C.s_w[slot], deps=[C.wfree[slot]] + list(deps))
        for mi in range(2):
            m = ti * 2 + mi
            b_ = next_bank(C)
            for k in range(16):
                last = MM(C, C.PS[b_][:, 0:256], wt[:, k, mi * 128:(mi + 1) * 128], memn[:, k, :], k == 0, k == 15,
                          ([wl, C.psfree[b_]] + nt) if k == 0 else (), sig=(k == 15))
            si = m % 2
            dst = KmT[:, m, :] if m < 4 else vT[:, m - 4, :]
            e2 = CP(C, "dve", dst, C.PS[b_][:, 0:256], [last])
            if MKSTOP[0] == 31:
                C.psfree[b_] = [e2]
                outs += [e2]
                continue
            e1 = ACT(C, stg[si], C.PS[b_][:, 0:256], AF.Copy, [last, stfree[si], e2])
            C.psfree[b_] = [e1, e2]
            if MKSTOP[0] == 32:
                outs += [e1, e2]
                continue
            stfree[si] = DMA(C, "sp", out_kvT[m], stg[si], C.s_stg[si], deps=[e1])
            outs += [stfree[si], e2]
        C.wfree[slot] = last
    if MKSTOP[0] in (3, 31, 32):
        return outs
    psb = C.PS[6][:, :].bitcast(BF16)
    tl = None
    for h in range(4):
        for mb in range(2):
            first = (h == 0 and mb == 0)
            tl = TR(C, psb[:, (mb * 4 + h) * 128:(mb * 4 + h + 1) * 128], vT[:, h, mb * 128:(mb + 1) * 128], C.identb[:],
                    (outs + [C.psfree[6]]) if first else (), sig=(h == 3 and mb == 1))
    cp = CP(C, "dve", Vm, psb[:, 0:1024].rearrange("p (a b) -> p a b", a=2), [tl])
    C.psfree[6] = cp
    return outs + [cp]


class StopHere(Exception):
    pass


STOP = [99]


def stage(C, n):
    if STOP[0] == n:
        C.S.barrier()
        raise StopHere()


def layer0(C, dr, xsrc, memkv_mode="compute", next_norm=None):
    S = C.S
    sx = newld(C)
    lds = [DMA(C, "sp", C.Bf[:, c, 0:NT], xsrc[c], sx) for c in range(16)]
    norm_to_H(C, 0, 0, lds)
    S.barrier()
    stage(C, 1)
    KmT = rview(C.RB, 57344, BF16, [128, 4, 256])
    Vm = rview(C.RB, 59392, BF16, [128, 2, 512])
    KsT = rview(C.RB, 61440, BF16, [128, 4, 256])
    Vs = rview(C.RB, 63488, BF16, [128, 2, 512])
    bsb = rview(C.RB, 45056, F32, [128, 12, 128])
    mkv = rview(C.RB, 57344, BF16, [128, 2048])
    if memkv_mode == "load":
        mk = [DMA(C, "sp", mkv, dr["mkscr"], newld(C))]
    else:
        mk = memkv_compute(C, 0, dr, dr["o_memkvT"], KmT, Vm, 0, [])
        if "mkscr" in dr:
            DMA(C, "sp", dr["mkscr"], mkv, C.s_st, deps=list(mk))
    s1, s2 = newld(C, "pool"), newld(C, "sp")
    mk.append(DMA(C, "pool", KsT, dr["cmemKT"][0].rearrange("h p m -> p h m"), s1))
    mk.append(DMA(C, "pool", Vs, dr["cmemV"][0].rearrange("(b p) c -> p b c", p=128), s1))
    for j in range(12):
        mk.append(DMA(C, "sp", bsb[:, j, :], dr["b_s"][j // 3].partition_broadcast(128), s2))
    if memkv_mode != "load":
        S.barrier()
    stage(C, 2)
    ring = ring_views(C.RB, 0, 3, 8192, 16, 256)
    gt = [rview(C.RB, 24576 + i * NTP * 4, F32, [128, NTP]) for i in range(4)]
    gfree = [None] * 4
    vsamp = C.small[:, 16:28]
    cnt = [0]

    def evac_in(col0, tbi, ps, tok):
        t0, tn = TB[tbi]
        m = col0 // 128
        if m >= 24:
            return CP(C, "act", C.Zb[:, m, t0:t0 + tn], ps, [tok])
        i = cnt[0] % 4
        cnt[0] += 1
        g = gt[i][:, 0:tn]
        a = ACT(C, g, ps, AF.Square, [tok, gfree[i]])
        b = TS(C, g, g, 0.044715, 1.0, ALU.mult, ALU.add, [a])
        c_ = TT(C, g, g, ps, ALU.mult, [b])
        d = ACT(C, g, g, AF.Sigmoid, [c_], scale=GELU_C)
        r = TT(C, C.Zb[:, m, t0:t0 + tn], g, ps, ALU.mult, [d])
        if 12 <= m < 24 and tbi == 2:
            r = TT(C, vsamp[:, m - 12:m - 11], gt[i][:, 340:341], ps[:, 340:341], ALU.mult, [d, r])
        gfree[i] = r
        return r

    mm_fm(C, dr["w_in_a"], 16, [(256 * t, 256) for t in range(14)], ring, lambda k, t0, tn: C.Hb[:, k, t0:t0 + tn], evac_in, [])
    S.barrier()
    stage(C, 3)
    rstdv = C.tmp[0]
    t = sumsq_rstd(C, lambda c: C.Zb[:, 12 + c, 0:NT], 12, float(MIXW), rstdv, [])
    vt = [STT(C, C.Zb[:, 12 + j, 0:NT], C.Zb[:, 12 + j, 0:NT], gcol(C, 160 + j), rstdv[:, 0:NT], ALU.mult, ALU.mult, [t])
          for j in range(12)]
    vso = C.small[:, 28:40]
    s1 = TT(C, vso, vsamp, C.gcols[:, 160:172], ALU.mult, [t])
    s2 = TS1(C, vso, vso, rstdv[:, 1024:1025], ALU.mult, [s1])
    DMA(C, "sp", dr["o_vrows"], vso, C.s_st, deps=[s2])
    vtm = rview(C.RZ, 28 * NTP * 2, BF16, [128, 9, MIXW])
    pa = C.PS[6][:, :].bitcast(BF16)
    pb = C.PS[7][:, :].bitcast(BF16)
    for ti in range(9):
        t0 = ti * 128
        tn = 128 if ti < 8 else 1
        la = lb = None
        for j in range(12):
            pp, jj, bk = (pa, j, 6) if j < 8 else (pb, j - 8, 7)
            x = TR(C, pp[0:tn, jj * 128:(jj + 1) * 128], C.Zb[:, 12 + j, t0:t0 + tn], C.identb[:],
                   (vt + [C.psfree[bk]]) if j in (0, 8) else (), sig=(j in (7, 11)))
            if j == 7:
                la = x
            if j == 11:
                lb = x
        C.psfree[6] = CP(C, "act", vtm[0:tn, ti, 0:1024], pa[0:tn, 0:1024], [la])
        C.psfree[7] = CP(C, "dve", vtm[0:tn, ti, 1024:1536], pb[0:tn, 0:512], [lb])
    vready = [C.psfree[6], C.psfree[7]]
    stage(C, 4)
    WmT = C.WmT
    g0 = rview(C.RB, 24576, F32, [128, 12, 128])
    g0free = None
    for ti in range(8):
        last = None
        for j in range(12):
            bk = j // 4
            last = MM(C, C.PS[bk][:, (j % 4) * 128:(j % 4 + 1) * 128], vtm[:, ti, j * 128:(j + 1) * 128], WmT[:, j // 3, :],
                      True, True, (vready + [C.psfree[bk]] + C.wm_ready) if j % 4 == 0 else (), sig=(j % 4 == 3))
            if j % 4 == 3:
                q = j // 4
                a = TT(C, g0[:, q * 4:(q + 1) * 4, :], C.PS[bk][:, :].rearrange("p (a b) -> p a b", a=4),
                       bsb[:, q * 4:(q + 1) * 4, :], ALU.add, [last, g0free] + mk)
                C.psfree[bk] = a
                b = TT(C, C.Hb[:, q * 4:(q + 1) * 4, ti * 128:(ti + 1) * 128], g0[:, q * 4:(q + 1) * 4, :],
                       C.Zb[:, q * 4:(q + 1) * 4, ti * 128:(ti + 1) * 128], ALU.mult, [a])
                if q == 2:
                    g0free = b
    last = None
    for j in range(12):
        last = MM(C, C.PS[3][:, j:j + 1], vtm[0:1, 8, j * 128:(j + 1) * 128], WmT[0:1, j // 3, 0:1], True, True,
                  (vready + [C.psfree[3]] + C.wm_ready) if j == 0 else (), sig=(j == 11))
    a = TT(C, g0[:, :, 0], C.PS[3][:, 0:12], bsb[:, :, 0], ALU.add, [last, g0free] + mk)
    C.psfree[3] = a
    sp_last = TT(C, C.Hb[:, 0:12, 1024], g0[:, :, 0], C.Zb[:, 0:12, 1024], ALU.mult, [a])
    stage(C, 5)
    mem_attention(C, 24, KmT, Vm, KsT, Vs, mk + [sp_last])
    S.barrier()
    stage(C, 6)
    ring = ring_views(C.RZ, 2 * NTP * 4, 3, 8192, 16, 256)

    def evac_o(col0, tbi, ps, tok):
        t0, tn = TB[tbi]
        return CP(C, evac_engine(C), C.Bf[:, col0 // 128, t0:t0 + tn], ps, [tok])

    mm_fm(C, dr["w_out"][0], 16, [(256 * t, 256) for t in range(8)], ring, lambda k, t0, tn: C.Hb[:, k, t0:t0 + tn], evac_o, [])
    S.barrier()
    stage(C, 7)
    post_norm_residual(C, 1, 0, lambda c: xsrc[c], [], next_norm=(2, 0), store_to=dr["x1scr"])
    S.barrier()
    stage(C, 8)
    ffn(C, 0, dr["w_ffn_up"][0], dr["w_ffn_down"][0], dr["x1scr"], prenormed=True, next_norm=next_norm)


def spatial_setup(C, dr):
    C.WmT = C.sb("WmT", [128, 4, 128], BF16)
    wtmp = rview(C.RZ, 0, F32, [128, 4, 128])
    mtmp = rview(C.RZ, 2048, F32, [128, 4, 128])
    s1 = newld(C)
    a = DMA(C, "sp", wtmp, dr["wsT"], s1)
    b = DMA(C, "sp", mtmp, dr["trilT"], s1)
    C.wm_ready = [TT(C, C.WmT[:], wtmp, mtmp, ALU.mult, [a, b])]


def dram_in(nc, name, shape):
    return nc.dram_tensor(name, list(shape), F32, kind="ExternalInput").ap()


def dram_out(nc, name, shape):
    return nc.dram_tensor(name, list(shape), F32, kind="ExternalOutput").ap()


def dump_debug(C, dr):
    S = C.S
    S.barrier()
    free = [None, None, None]
    n = 0
    for (src, cnt, base) in ((C.Zb, 44, 0), (C.Hb, 16, 44)):
        for c in range(cnt):
            i = n % 3
            n += 1
            a = CP(C, "dve", C.tmp[i][:, 0:NT], src[:, c, 0:NT], [free[i]])
            free[i] = DMA(C, "sp", dr["o_dbg"][base + c], C.tmp[i][:, 0:NT], C.s_stg[i], deps=[a])
    for c in range(16):
        DMA(C, "sp", dr["o_dbg"][60 + c], C.Bf[:, c, 0:NT], C.s_st)
    S.barrier()


def build_A():
    nc = bass.Bass("TRN2", target_bir_lowering=False)
    dr = {}
    if STOP[0] < 99:
        dr["o_dbg"] = dram_out(nc, "o_dbg", (76, 128, NT))
    for name, shape in [("xT", (16, 128, NT)), ("memT", (16, 128, 256)), ("cmemKT", (2, 4, 128, 256)), ("cmemV", (2, 256, 512)),
                        ("identf", (128, 128)), ("onesf", (128, 128)), ("gpack", (128, NG)), ("wsT", (128, 4, 128)),
                        ("trilT", (128, 4, 128)), ("b_s", (4, 128)), ("w_mem_kv", (2, D, 1024)), ("w_in_a", (D, 3584)),
                        ("w_out", (2, D, D)), ("w_ffn_up", (2, D, 2 * DFF)), ("w_ffn_down", (2, DFF, D)), ("w_kv_b", (D, 3072))]:
        dr[name] = dram_in(nc, name, shape)
    dr["o_memkvT"] = dram_out(nc, "o_memkvT", (8, 128, 256))
    dr["o_vrows"] = dram_out(nc, "o_vrows", (128, 12))
    dr["o_xmid"] = dram_out(nc, "o_xmid", (16, 128, NT))
    dr["o_kvT"] = dram_out(nc, "o_kvT", (24, 128, NT))
    dr["x1scr"] = nc.dram_tensor("x1scr", [16, 128, NT], F32).ap()
    with ExitStack() as st:
        C = setup(nc, st)
        S = C.S
        load_consts(C, dr)
        spatial_setup(C, dr)
        S.barrier()
        try:
            layer0(C, dr, dr["xT"])
        except StopHere:
            dump_debug(C, dr)
            with nc.Block() as block:
                S.emit(block)
            return nc
        store_B(C, dr["o_xmid"], [])
        norm_to_H(C, 0, 1, [])
        S.barrier()
        ring = ring_views(C.RZ, 0, 3, 8192, 16, 256)
        stg = [rview(C.RZ, 24576 + i * NTP * 4, F32, [128, NTP]) for i in range(4)]
        stfree = [None] * 4
        cnt = [0]

        def evac_kv(col0, tbi, ps, tok):
            t0, tn = TB[tbi]
            i = cnt[0] % 4
            cnt[0] += 1
            a = CP(C, evac_engine(C), stg[i][:, 0:tn], ps, [tok, stfree[i]])
            stfree[i] = DMA(C, "sp", dr["o_kvT"][col0 // 128][:, t0:t0 + tn], stg[i][:, 0:tn], C.s_stg[i], deps=[a])
            return a

        mm_fm(C, dr["w_kv_b"], 16, [(256 * t, 256) for t in range(12)], ring, lambda k, t0, tn: C.Hb[:, k, t0:t0 + tn], evac_kv, [])
        S.barrier()
        with nc.Block() as block:
            S.emit(block)
    return nc


def f32(a):
    return np.ascontiguousarray(np.asarray(a, dtype=np.float32))


def pack_gains(inp):
    rows = []
    for name in ("norm_mix_pre", "norm_mix_post", "norm_ffn_pre", "norm_ffn_post", "norm_mem"):
        g = np.asarray(inp[name], dtype=np.float32)
        for l in range(2):
            rows.append(g[l].reshape(16, 128))
    rows.append(np.asarray(inp["norm_v_a"], dtype=np.float32)[0].reshape(12, 128))
    return f32(np.concatenate(rows, axis=0).T)


def common_inputs(inp):
    ws = np.asarray(inp["w_spatial_a"], dtype=np.float32)[0]
    tril = np.tril(np.ones((128, 128), np.float32))
    return {
        "identf": np.eye(128, dtype=np.float32),
        "onesf": np.ones((128, 128), np.float32),
        "gpack": pack_gains(inp),
        "wsT": f32(ws.transpose(2, 0, 1)),
        "trilT": f32(np.broadcast_to(tril.T[:, None, :], (128, 4, 128))),
        "b_s": f32(np.asarray(inp["b_spatial_a"])[0]),
    }


def inputs_A(inp):
    xp = np.asarray(inp["x_prompt"], dtype=np.float32)
    xs = np.asarray(inp["x_sample"], dtype=np.float32)
    mem = np.asarray(inp["mem_prompt"], dtype=np.float32)
    cmem = np.asarray(inp["cache_mem_kv"], dtype=np.float32)
    w_in_b = np.asarray(inp["w_in_b"], dtype=np.float32)[0]
    shared = common_inputs(inp)
    shared.update({
        "w_mem_kv": f32(inp["w_mem_kv"]), "w_in_a": f32(np.asarray(inp["w_in_a"])[0]), "w_out": f32(inp["w_out"]),
        "w_ffn_up": f32(inp["w_ffn_up"]), "w_ffn_down": f32(inp["w_ffn_down"]),
        "w_kv_b": f32(w_in_b[:, 1536:4608]),
    })
    in_maps = []
    for c in range(8):
        b, p = c // 4, c % 4
        xtok = np.concatenate([xp[b, p * 1024:(p + 1) * 1024], xs[c]], axis=0)
        m = dict(shared)
        m["xT"] = f32(xtok.T.reshape(16, 128, NT))
        m["memT"] = f32(mem[b].T.reshape(16, 128, 256))
        m["cmemKT"] = f32(cmem[:, c, :, 0].transpose(0, 2, 3, 1))
        m["cmemV"] = f32(cmem[:, c, :, 1].reshape(2, 256, 512))
        in_maps.append(m)
    return in_maps


def run_A(inp):
    nc = build_A()
    res = run_bass_kernel_spmd(nc, inputs_A(inp), core_ids=list(range(8)))
    return res.results


GROUPS = ((128, 1), (512, 4), (2048, 16))


def tok_slice(r, b, bs, d, nblk=1):
    start = b * bs * d + r
    n = bs * nblk
    return slice(start, start + (n - 1) * d + 1, d)


def bias_setup(C, dr):
    rb = rview(C.RZ, 0, F32, [128, 12])
    Sg = rview(C.RZ, 64, F32, [128, 3, 384])
    vm = rview(C.RZ, 64 + 4608, F32, [128, 384])
    gt = rview(C.RZ, 64 + 4608 + 1536, F32, [128, 3, 384])
    s1 = newld(C)
    l1 = DMA(C, "sp", rb[0:32, :], dr["rel_bias"], s1)
    l2 = DMA(C, "sp", Sg[0:32, :, :], dr["Soh"].rearrange("g b m -> b g m"), s1)
    l3 = DMA(C, "sp", vm[0:4, :], dr["vmask"], s1)
    l1 = l2 = l3 = [l1, l2, l3]
    outs = []
    for g in range(3):
        m = MM(C, C.PS[g][0:4, 0:384], rb[0:32, 4 * g:4 * g + 4], Sg[0:32, g, :], True, True, [l1, l2, C.psfree[g]], sig=True)
        a = ACT(C, gt[0:4, g, :], C.PS[g][0:4, 0:384], AF.Exp, [m])
        C.psfree[g] = a
        b = TT(C, gt[0:4, g, :], gt[0:4, g, :], vm[0:4, :], ALU.mult, [a, l3])
        outs.append(DMA(C, "sp", dr["Gscr"][4 * g:4 * g + 4, :], gt[0:4, g, :], C.s_st, deps=[b]))
    C.Jf = C.sb("Jf", [128, 128], F32)
    outs.append(DMA(C, "sp", C.Jf[:], dr["Jf"], newld(C)))
    return outs


def attention_l1(C, dr, deps):
    S = C.S
    RB = C.RB
    off = [0]

    def alloc(dt, shape):
        v = rview(RB, off[0], dt, shape)
        esz = 4 if dt == F32 else 2
        off[0] += int(np.prod(shape[1:])) * esz
        return v

    oun = [alloc(F32, [128, NTP]) for _ in range(3)]
    den = [alloc(F32, [128, NTP]) for _ in range(3)]
    cst = [alloc(F32, [128, NTP]) for _ in range(3)]
    Et = [alloc(F32, [128, 256]) for _ in range(3)]
    Ep1 = alloc(F32, [128, 256])
    Ep = [Ep1, Ep1, Ep1]
    vtm = [alloc(BF16, [128, 9, 128]), alloc(BF16, [128, 8, 128]), alloc(BF16, [128, 16, 128])]
    vself = alloc(BF16, [128, 3, 128])
    hkn = [alloc(BF16, [128, 128]), alloc(BF16, [128, 512]), alloc(BF16, [128, 2048])]
    ckt = [alloc(BF16, [128, 128]) for _ in range(3)]
    hv = [alloc(BF16, [128, 2, 128]), alloc(BF16, [128, 5, 128]), alloc(BF16, [128, 17, 128])]
    hvn = C.tmp[2][:, :].bitcast(BF16)
    assert off[0] <= 16 * NTP * 4, off[0]
    hvc = C.small[:, 40:44]
    kmx = C.small[:, 44:48]
    ldv = DMA(C, "sp", hvc, dr["hvalid"], newld(C))
    t0_, t1_, t2_ = C.tmp
    prev = list(deps) + [ldv]
    SBK = (0, 1, 2, 3)
    sbi = [0]
    pbf = C.PS[3][:, :].bitcast(BF16)
    for h in range(4):
        lds = []
        sp_, se_ = newld(C, "pool"), newld(C, "sp")
        for g, (win, d) in enumerate(GROUPS):
            hh = 4 * g + h
            nh = 128 * d
            lds.append(DMA(C, "pool", hkn[g], dr["KVh"][hh][:, 2048 - nh:2048], sp_, deps=prev))
            lds.append(DMA(C, "pool", ckt[g], dr[f"cK{g}"][h], sp_, deps=prev))
            lds.append(DMA(C, "pool", hv[g][:, d, :], dr[f"cV{g}"][:, h * 128:(h + 1) * 128], sp_, deps=prev))
        lds = [lds]
        hvfree = prev
        epfree = None
        for g, (win, d) in enumerate(GROUPS):
            hh = 4 * g + h
            nh = 128 * d
            lv = DMA(C, "pool", hvn[:, 0:nh], dr["KVh"][12 + hh][:, 2048 - nh:2048], newld(C, "pool"), deps=hvfree)
            x = None
            for i0 in range(0, d, 8):
                n = min(8, d - i0)
                for i in range(n):
                    r = i0 + i
                    x = TR(C, pbf[:, i * 128:(i + 1) * 128], hvn[:, r:r + 127 * d + 1:d], C.identb[:],
                           ([lv, C.psfree[3]] + prev) if i == 0 else (), sig=(i == n - 1))
                e = CP(C, "act", hv[g][:, i0:i0 + n, :], pbf[:, 0:n * 128].rearrange("p (a b) -> p a b", a=n), [x] + prev)
                C.psfree[3] = e
                lds.append(e)
            hvfree = [x]
            src = bass.AP(dr["Gscr"].tensor, hh * 384, [[1, 128], [1, 256]])
            bk = SBK[sbi[0] % 3]
            sbi[0] += 1
            le = DMA(C, "sp", Ep[g], src, se_, deps=prev + C.gs_ready + [epfree])
            m = MM(C, C.PS[bk][:, 0:256], C.Jf[:], Ep[g], True, True, [le, C.psfree[bk]], sig=True)
            epfree = m
            e = CP(C, "dve", Et[g], C.PS[bk][:, 0:256], [m] + prev)
            C.psfree[bk] = e
            lds.append(e)
        readys = []
        for g, (win, d) in enumerate(GROUPS):
            hh = 4 * g + h
            R = d
            L = 1024 // d
            bs = min(128, L)
            nb = L // bs
            qT = C.Zb[:, hh, :]
            kT = C.Zb[:, 12 + hh, :]
            vT = C.Zb[:, 24 + hh, :]
            nhk = (R + 1) * 128
            mx = []
            a = ACT(C, C.sq[0][:, 0:NT], kT[:, 0:NT], AF.Square, prev)
            for tbi, (t0, tn) in enumerate(TB):
                m = MM(C, C.PS[tbi][:, 0:tn], C.onesb[:], C.sq[0][:, t0:t0 + tn], True, True, [a, C.psfree[tbi]], sig=True)
                r = RMAX(C, C.small[:, 48 + len(mx):49 + len(mx)], C.PS[tbi][:, 0:tn], [m])
                C.psfree[tbi] = r
                mx.append(r)
                prev = prev + [m]
            ksrcs = [hkn[g][:, c0:c0 + min(512, R * 128 - c0)] for c0 in range(0, R * 128, 512)] + [ckt[g][:, :]]
            for ksrc in ksrcs:
                cn = ksrc.shape[1]
                a = ACT(C, C.sq[1][:, 0:cn], ksrc, AF.Square, prev + lds)
                bk = SBK[sbi[0] % 3]
                sbi[0] += 1
                m = MM(C, C.PS[bk][:, 0:cn], C.onesb[:], C.sq[1][:, 0:cn], True, True, [a, C.psfree[bk]], sig=True)
                r = RMAX(C, C.small[:, 48 + len(mx):49 + len(mx)], C.PS[bk][:, 0:cn], [m])
                C.psfree[bk] = r
                mx.append(r)
                prev = prev + [m]
            km = RMAX(C, kmx[:, g:g + 1], C.small[:, 48:48 + len(mx)], mx)
            a = ACT(C, C.sq[0][:, 0:NT], qT[:, 0:NT], AF.Square, prev)
            ctoks = []
            for tbi, (t0, tn) in enumerate(TB):
                m = MM(C, C.PS[tbi][:, 0:tn], C.onesb[:], C.sq[0][:, t0:t0 + tn], True, True, [a, C.psfree[tbi]], sig=True)
                r = TS(C, cst[g][:, t0:t0 + tn], C.PS[tbi][:, 0:tn], kmx[:, g:g + 1], 0.5 * SCALE, ALU.add, ALU.mult, [m, km])
                C.psfree[tbi] = r
                ctoks.append(r)
                prev = prev + [m]
            vt_toks = []
            blocks = [(r, b) for r in range(R) for b in range(nb)]
            if g == 0:
                blocks = [(0, b) for b in range(8)]
            for i0 in range(0, len(blocks), 8):
                grp = blocks[i0:i0 + 8]
                x = None
                for i, (r, b) in enumerate(grp):
                    x = TR(C, pbf[0:bs, i * 128:(i + 1) * 128], vT[:, tok_slice(r, b, bs, d)], C.identb[:],
                           (prev + [C.psfree[3]]) if i == 0 else (), sig=(i == len(grp) - 1))
                e = CP(C, "act", vtm[g][0:bs, i0:i0 + len(grp), :], pbf[0:bs, 0:len(grp) * 128].rearrange("p (a b) -> p a b", a=len(grp)), [x])
                C.psfree[3] = e
                vt_toks.append(e)
            x = TR(C, pbf[0:1, 0:128], vT[:, 1024:1025], C.identb[:], prev + [C.psfree[3]], sig=True)
            e = CP(C, "act", vself[0:1, g, :], pbf[0:1, 0:128], [x])
            C.psfree[3] = e
            vt_toks.append(e)
            readys.append(prev + lds + ctoks + vt_toks)
        for g, (win, d) in enumerate(GROUPS):
            hh = 4 * g + h
            R = d
            L = 1024 // d
            bs = min(128, L)
            nb = L // bs
            qT = C.Zb[:, hh, :]
            kT = C.Zb[:, 12 + hh, :]
            ready = readys[g] + prev
            units = []
            for r in range(R + 1):
                sample = (r == R)
                rbs = 1 if sample else bs
                rnb = 1 if sample else nb
                for kb in range(-1, rnb):
                    u = dict(r=r, kb=kb, sample=sample, rbs=rbs)
                    if kb < 0:
                        u["K"] = ckt[g][:, :] if sample else hkn[g][:, r:r + 127 * d + 1:d]
                        u["nk"] = 128
                        u["V"] = hv[g][:, r, :]
                        u["qsl"] = slice(1024, 1025) if sample else tok_slice(r, 0, bs, d)
                        u["nq"] = rbs
                        u["qbs"] = [0]
                    else:
                        if sample:
                            u["K"] = kT[:, 1024:1025]
                            u["V"] = vself[0:1, g, :]
                            u["qsl"] = slice(1024, 1025)
                        else:
                            u["K"] = kT[:, tok_slice(r, kb, bs, d)]
                            u["V"] = vtm[g][0:bs, (r * nb + kb) if g else kb, :]
                            u["qsl"] = tok_slice(r, kb, bs, d, 2 if kb + 1 < rnb else 1)
                        u["nk"] = rbs
                        u["qbs"] = [kb, kb + 1] if kb + 1 < rnb else [kb]
                        u["nq"] = rbs * len(u["qbs"])
                    units.append(u)
            n = len(units)
            tA = [t0_[:, 256 * j:256 * (j + 1)] for j in range(4)]
            tB = [t1_[:, 256 * j:256 * (j + 1)] for j in range(4)]
            PTs = [C.sq[j // 4][:, 256 * (j % 4):256 * (j % 4 + 1)] for j in range(8)]
            tAfree = [None] * 4
            tBfree = [None] * 4
            PTfree = [None] * 8
            atok = [None] * n
            etok = [None] * n
            lastpe = None
            mtok = [None] * n
            ptok = [None] * n
            pend = [None] * n
            SB4 = (0, 1)
            for i in range(n + 5):
                if i < n:
                    u = units[i]
                    bk = SB4[i % 2]
                    mtok[i] = MM(C, C.PS[bk][0:u["nk"], 0:u["nq"]], u["K"], qT[:, u["qsl"]], True, True,
                                 ready + [C.psfree[bk]], sig=True)
                j = i - 1
                if 0 <= j < n:
                    u = units[j]
                    nk, nq = u["nk"], u["nq"]
                    bk = SB4[j % 2]
                    atok[j] = STT(C, tA[j % 4][0:nk, 0:nq], C.PS[bk][0:nk, 0:nq], SCALE, cst[g][0:nk, u["qsl"]],
                                  ALU.mult, ALU.subtract, [mtok[j], tAfree[j % 4]])
                    C.psfree[bk] = atok[j]
                j = i - 2
                if 0 <= j < n:
                    u = units[j]
                    nk, nq = u["nk"], u["nq"]
                    etok[j] = ACT(C, tB[j % 4][0:nk, 0:nq], tA[j % 4][0:nk, 0:nq], AF.Exp, [atok[j], tBfree[j % 4]])
                    tAfree[j % 4] = etok[j]
                j = i - 3
                if 0 <= j < n:
                    u = units[j]
                    nk, nq, kb, qbs, sample = u["nk"], u["nq"], u["kb"], u["qbs"], u["sample"]
                    PT = PTs[j % 8]
                    e = etok[j]
                    if kb < 0:
                        vc = hvc[:, 2:3] if sample else (hvc[:, 1:2] if g == 2 else hvc[:, 0:1])
                        p = STT(C, PT[0:nk, 0:nq], tB[j % 4][0:nk, 0:nq], vc, Et[g][0:nk, 128:128 + nq], ALU.mult, ALU.mult,
                                [e, PTfree[j % 8]])
                    elif len(qbs) == 2:
                        p = TT(C, PT[0:nk, 0:nq], tB[j % 4][0:nk, 0:nq], Et[g][0:nk, 0:256], ALU.mult, [e, PTfree[j % 8]])
                    else:
                        p = TT(C, PT[0:nk, 0:nq], tB[j % 4][0:nk, 0:nq], Et[g][0:nk, 0:nq], ALU.mult, [e, PTfree[j % 8]])
                    tBfree[j % 4] = p
                    ptok[j] = p
                j = i - 5
                if 0 <= j < n and pend[j]:
                    u = units[j]
                    rbs, r, sample = u["rbs"], u["r"], u["sample"]
                    for (qb, ob, db, last) in pend[j]:
                        osl = slice(1024, 1025) if sample else tok_slice(r, qb, bs, d)
                        C.psfree[ob] = CP(C, "act", oun[g][:, osl], C.PS[ob][:, 0:rbs], [last])
                        C.psfree[db] = CP(C, "dve", den[g][:, osl], C.PS[db][:, 0:rbs], [last])
                j = i - 4
                if 0 <= j < n:
                    u = units[j]
                    nk, rbs, r, kb, qbs = u["nk"], u["rbs"], u["r"], u["kb"], u["qbs"]
                    PT = PTs[j % 8]
                    p = ptok[j]
                    last = None
                    fin = []
                    for qi, qb in enumerate(qbs):
                        first = (kb == qb - 1)
                        final = (kb == qb)
                        ob = 2 + ((qb + r) % 3)
                        db = 5 + ((qb + r) % 3)
                        cols = slice(qi * rbs, (qi + 1) * rbs)
                        MM(C, C.PS[ob][:, 0:rbs], u["V"], PT[0:nk, cols], first, final, [p, C.psfree[ob]] if first else [p])
                        last = MM(C, C.PS[db][:, 0:rbs], C.onesb[0:nk, :], PT[0:nk, cols], first, final,
                                  [p, C.psfree[db]] if first else [p], sig=True)
                        if final:
                            fin.append((qb, ob, db, last))
                    PTfree[j % 8] = last
                    lastpe = last
                    pend[j] = fin
            prev = prev + [lastpe] + [x for x in etok[-4:]] + [x for x in atok[-3:]]
            prev = prev + [C.psfree[b_] for b_ in range(2, 8)]
        a = TT(C, t0_[:, 0:NT], cst[0][:, 0:NT], cst[1][:, 0:NT], ALU.max, prev)
        a = TT(C, t0_[:, 0:NT], t0_[:, 0:NT], cst[2][:, 0:NT], ALU.max, [a])
        ws = []
        for g in range(3):
            b = TT(C, cst[g][:, 0:NT], cst[g][:, 0:NT], t0_[:, 0:NT], ALU.subtract, [a])
            b = ACT(C, cst[g][:, 0:NT], cst[g][:, 0:NT], AF.Exp, [b])
            b = TT(C, den[g][:, 0:NT], den[g][:, 0:NT], cst[g][:, 0:NT], ALU.mult, [b])
            ws.append(b)
        b = TT(C, t1_[:, 0:NT], den[0][:, 0:NT], den[1][:, 0:NT], ALU.add, ws)
        b = TT(C, t1_[:, 0:NT], t1_[:, 0:NT], den[2][:, 0:NT], ALU.add, [b])
        b = RCP(C, t1_[:, 0:NT], t1_[:, 0:NT], [b])
        outs = []
        for g in range(3):
            x = TT(C, cst[g][:, 0:NT], cst[g][:, 0:NT], t1_[:, 0:NT], ALU.mult, [b])
            outs.append(TT(C, C.Hb[:, 4 * g + h, 0:NT], oun[g][:, 0:NT], cst[g][:, 0:NT], ALU.mult, [x]))
        prev = prev + outs
    return prev


def layer1(C, dr, xsrc, memkv_key="o_memkvT", preloaded=False):
    S = C.S
    if not preloaded:
        sx = newld(C)
        lds = [DMA(C, "sp", C.Bf[:, c, 0:NT], xsrc[c], sx) for c in range(16)]
        norm_to_H(C, 0, 1, lds)
    S.barrier()
    zo = 40 * NTP * 2
    KmT = rview(C.RZ, zo, BF16, [128, 4, 256])
    Vm = rview(C.RZ, zo + 2048, BF16, [128, 2, 512])
    KsT = rview(C.RZ, zo + 4096, BF16, [128, 4, 256])
    Vs = rview(C.RZ, zo + 6144, BF16, [128, 2, 512])
    mk = memkv_compute(C, 1, dr, dr[memkv_key], KmT, Vm, 0, [])
    s1 = newld(C, "pool")
    mk.append(DMA(C, "pool", KsT, dr["cmemKT"][1].rearrange("h p m -> p h m"), s1))
    mk.append(DMA(C, "pool", Vs, dr["cmemV"][1].rearrange("(b p) c -> p b c", p=128), s1))
    S.barrier()
    ring = ring_views(C.RB, 0, 3, 8192, 16, 256)

    def evac_in(col0, tbi, ps, tok):
        t0, tn = TB[tbi]
        return CP(C, evac_engine(C), C.Zb[:, col0 // 128, t0:t0 + tn], ps, [tok])

    mm_fm(C, dr["w_in_b"], 16, [(256 * t, 256) for t in range(20)], ring, lambda k, t0, tn: C.Hb[:, k, t0:t0 + tn], evac_in, [])
    S.barrier()
    p = attention_l1(C, dr, [])
    mem_attention(C, 36, KmT, Vm, KsT, Vs, p)
    S.barrier()
    ring = ring_views(C.RZ, 2 * NTP * 4, 3, 8192, 16, 256)

    def evac_o(col0, tbi, ps, tok):
        t0, tn = TB[tbi]
        return CP(C, evac_engine(C), C.Bf[:, col0 // 128, t0:t0 + tn], ps, [tok])

    mm_fm(C, dr["w_out"][1], 16, [(256 * t, 256) for t in range(8)], ring, lambda k, t0, tn: C.Hb[:, k, t0:t0 + tn], evac_o, [])
    S.barrier()
    post_norm_residual(C, 1, 1, lambda c: xsrc[c], [], next_norm=(2, 1), store_to=dr["x1scr"])
    S.barrier()
    ffn(C, 1, dr["w_ffn_up"][1], dr["w_ffn_down"][1], dr["x1scr"], prenormed=True)


def build_B():
    nc = bass.Bass("TRN2", target_bir_lowering=False)
    dr = {}
    for name, shape in [("xT", (16, 128, NT)), ("memT", (16, 128, 256)), ("cmemKT", (2, 4, 128, 256)), ("cmemV", (2, 256, 512)),
                        ("identf", (128, 128)), ("onesf", (128, 128)), ("gpack", (128, NG)), ("Jf", (128, 128)),
                        ("rel_bias", (32, 12)), ("Soh", (3, 32, 384)), ("vmask", (4, 384)), ("hvalid", (128, 4)),
                        ("KVh", (24, 128, 2048)), ("cK0", (4, 128, 128)), ("cK1", (4, 128, 128)), ("cK2", (4, 128, 128)),
                        ("cV0", (128, 512)), ("cV1", (128, 512)), ("cV2", (128, 512)),
                        ("w_mem_kv", (2, D, 1024)), ("w_in_b", (D, 5120)),
                        ("w_out", (2, D, D)), ("w_ffn_up", (2, D, 2 * DFF)), ("w_ffn_down", (2, DFF, D))]:
        dr[name] = dram_in(nc, name, shape)
    dr["o_memkvT"] = dram_out(nc, "o_memkvT", (8, 128, 256))
    dr["o_y"] = dram_out(nc, "o_y", (16, 128, NT))
    dr["x1scr"] = nc.dram_tensor("x1scr", [16, 128, NT], F32).ap()
    dr["Gscr"] = nc.dram_tensor("Gscr", [12, 384], F32).ap()
    with ExitStack() as st:
        C = setup(nc, st)
        S = C.S
        load_consts(C, dr)
        C.gs_ready = bias_setup(C, dr)
        S.barrier()
        layer1(C, dr, dr["xT"])
        store_B(C, dr["o_y"], [])
        S.barrier()
        with nc.Block() as block:
            S.emit(block)
    return nc


def t5_bucket_np(dist):
    nf = np.maximum(dist, 16).astype(np.float32)
    large = 16 + (np.log(nf / 16) / np.log(2048 / 16) * 16).astype(np.int32)
    large = np.minimum(large, 31)
    return np.where(dist < 16, dist, large)


def bias_tables():
    Soh = np.zeros((3, 32, 384), np.float32)
    vmask = np.zeros((4, 384), np.float32)
    for g, (win, d) in enumerate(GROUPS):
        j = np.arange(129, dtype=np.int32)
        bk = t5_bucket_np(j * d)
        Soh[g, bk, j + 127] = 1.0
    vmask[:, 127:256] = 1.0
    return Soh, vmask


def cache_inputs(caches, c):
    m = {}
    i = np.arange(128)
    for g, (win, d) in enumerate(GROUPS):
        rows = caches[g][c][i * d]
        m[f"cK{g}"] = f32(rows[:, 0].transpose(1, 2, 0))
        m[f"cV{g}"] = f32(rows[:, 1].reshape(128, 512))
    return m


def inputs_B(inp, ra):
    mem = np.asarray(inp["mem_prompt"], dtype=np.float32)
    cmem = np.asarray(inp["cache_mem_kv"], dtype=np.float32)
    caches = [np.asarray(inp[n], dtype=np.float32)[0] for n in ("cache_win128_kv", "cache_win512_kv", "cache_win2048_kv")]
    Soh, vmask = bias_tables()
    shared = {
        "identf": np.eye(128, dtype=np.float32), "onesf": np.ones((128, 128), np.float32), "gpack": pack_gains(inp),
        "Jf": f32(np.eye(128, dtype=np.float32)[::-1]), "rel_bias": f32(inp["rel_bias"]), "Soh": Soh, "vmask": vmask,
        "w_mem_kv": f32(inp["w_mem_kv"]), "w_in_b": f32(np.asarray(inp["w_in_b"])[0]), "w_out": f32(inp["w_out"]),
        "w_ffn_up": f32(inp["w_ffn_up"]), "w_ffn_down": f32(inp["w_ffn_down"]),
    }
    KT = []
    VT = []
    for b in range(2):
        KT.append(np.concatenate([ra[4 * b + p]["o_kvT"][0:12, :, 0:1024] for p in range(4)], axis=2))
        VT.append(np.concatenate([ra[4 * b + p]["o_kvT"][12:24, :, 0:1024] for p in range(4)], axis=2))
    in_maps = []
    for c in range(8):
        b, p = c // 4, c % 4
        s = p * 1024
        m = dict(shared)
        m["xT"] = f32(ra[c]["o_xmid"])
        m["memT"] = f32(mem[b].T.reshape(16, 128, 256))
        m["cmemKT"] = f32(cmem[:, c, :, 0].transpose(0, 2, 3, 1))
        m["cmemV"] = f32(cmem[:, c, :, 1].reshape(2, 256, 512))
        hvalid = np.zeros((128, 4), np.float32)
        hvalid[:, 0] = 1.0 if p >= 1 else 0.0
        hvalid[:64, 1] = 1.0 if p >= 2 else 0.0
        hvalid[64:, 1] = 1.0 if p >= 1 else 0.0
        hvalid[:, 2] = 1.0
        m["hvalid"] = hvalid
        KVh = np.zeros((24, 128, 2048), np.float32)
        n = min(s, 2048)
        if n > 0:
            KVh[0:12, :, 2048 - n:] = KT[b][:, :, s - n:s]
            KVh[12:24, :, 2048 - n:] = VT[b][:, :, s - n:s]
        m["KVh"] = KVh
        m.update(cache_inputs(caches, c))
        in_maps.append(m)
    return in_maps


def run_B(inp, ra):
    nc = build_B()
    res = run_bass_kernel_spmd(nc, inputs_B(inp, ra), core_ids=list(range(8)))
    return res.results


def assemble(inp, ra, rb):
    y_prompt = np.zeros((2, 4096, D), np.float32)
    y_sample = np.zeros((8, 1, D), np.float32)
    memkv = np.zeros((2, 2, 256, 2, 4, 128), np.float32)
    vrows = np.zeros((1, 8, 1, MIXW), np.float32)
    winp = [np.zeros((1, 2, min(w, 4096), 2, 4, 128), np.float32) for (w, d) in GROUPS]
    wins = [np.zeros((1, 8, 1, 2, 4, 128), np.float32) for _ in GROUPS]
    for c in range(8):
        b, p = c // 4, c % 4
        y = rb[c]["o_y"].reshape(D, NT).T
        y_prompt[b, p * 1024:(p + 1) * 1024] = y[:1024]
        y_sample[c, 0] = y[1024]
        vrows[0, c, 0] = ra[c]["o_vrows"].T.reshape(MIXW)
        kvT = ra[c]["o_kvT"]
        for g in range(3):
            wins[g][0, c, 0, 0] = kvT[4 * g:4 * g + 4, :, 1024]
            wins[g][0, c, 0, 1] = kvT[12 + 4 * g:16 + 4 * g, :, 1024]
    for b in range(2):
        memkv[0, b] = ra[4 * b]["o_memkvT"].reshape(1024, 256).T.reshape(256, 2, 4, 128)
        memkv[1, b] = rb[4 * b]["o_memkvT"].reshape(1024, 256).T.reshape(256, 2, 4, 128)
        KT = np.concatenate([ra[4 * b + p]["o_kvT"][0:12, :, 0:1024] for p in range(4)], axis=2)
        VT = np.concatenate([ra[4 * b + p]["o_kvT"][12:24, :, 0:1024] for p in range(4)], axis=2)
        for g, (w, d) in enumerate(GROUPS):
            n = min(w, 4096)
            winp[g][0, b, :, 0] = KT[4 * g:4 * g + 4, :, 4096 - n:].transpose(2, 0, 1)
            winp[g][0, b, :, 1] = VT[4 * g:4 * g + 4, :, 4096 - n:].transpose(2, 0, 1)
    return (y_prompt, y_sample, memkv, vrows, winp[0], winp[1], winp[2], wins[0], wins[1], wins[2])


TBH = [(0, 342), (342, 342), (684, 340)]


def kv_proj(C, dr, coltiles, dst, tcol0, tb):
    ring = ring_views(C.RZ, 0, 3, 8192, 16, 256)
    stg = [rview(C.RZ, 24576 + i * NTP * 4, F32, [128, NTP]) for i in range(4)]
    stfree = [None] * 4
    cnt = [0]
    W = dr["w_in_b"][:, 1536:4608]

    def evac_kv(col0, tbi, ps, tok):
        t0, tn = tb[tbi]
        i = cnt[0] % 4
        cnt[0] += 1
        a = CP(C, evac_engine(C), stg[i][:, 0:tn], ps, [tok, stfree[i]])
        stfree[i] = DMA(C, "sp", dst[col0 // 128][:, tcol0 + t0:tcol0 + t0 + tn], stg[i][:, 0:tn], C.s_stg[i], deps=[a])
        return a

    mm_fm(C, W, 16, coltiles, ring, lambda k, t0, tn: C.Hb[:, k, t0:t0 + tn], evac_kv, [], tb=tb)


def build_F():
    nc = bass.Bass("TRN2", target_bir_lowering=False)
    dr = {}
    for name, shape in [("xT", (16, 128, NT)), ("xTh", (2, 16, 128, NT)), ("memT", (16, 128, 256)), ("cmemKT", (2, 4, 128, 256)),
                        ("cmemV", (2, 256, 512)), ("identf", (128, 128)), ("onesf", (128, 128)), ("gpack", (128, NG)),
                        ("wsT", (128, 4, 128)), ("trilT", (128, 4, 128)), ("b_s", (4, 128)), ("Jf", (128, 128)),
                        ("rel_bias", (32, 12)), ("Soh", (3, 32, 384)), ("vmask", (4, 384)), ("hvalid", (128, 4)),
                        ("cK0", (4, 128, 128)), ("cK1", (4, 128, 128)), ("cK2", (4, 128, 128)),
                        ("cV0", (128, 512)), ("cV1", (128, 512)), ("cV2", (128, 512)),
                        ("w_mem_kv", (2, D, 1024)), ("w_in_a", (D, 3584)), ("w_in_b", (D, 5120)),
                        ("w_out", (2, D, D)), ("w_ffn_up", (2, D, 2 * DFF)), ("w_ffn_down", (2, DFF, D))]:
        dr[name] = dram_in(nc, name, shape)
    dr["o_memkvT"] = dram_out(nc, "o_memkvT", (8, 128, 256))
    dr["o_memkvT1"] = dram_out(nc, "o_memkvT1", (8, 128, 256))
    dr["o_vrows"] = dram_out(nc, "o_vrows", (128, 12))
    dr["o_kvT"] = dram_out(nc, "o_kvT", (24, 128, NT))
    dr["o_y"] = dram_out(nc, "o_y", (16, 128, NT))
    dr["x1scr"] = nc.dram_tensor("x1scr", [16, 128, NT], F32).ap()
    dr["xmid"] = nc.dram_tensor("xmid", [16, 128, NT], F32).ap()
    dr["KVh"] = nc.dram_tensor("KVh", [24, 128, 2048], F32).ap()
    dr["Gscr"] = nc.dram_tensor("Gscr", [12, 384], F32).ap()
    dr["mkscr"] = nc.dram_tensor("mkscr", [128, 2048], BF16).ap()
    with ExitStack() as st:
        C = setup(nc, st)
        S = C.S
        load_consts(C, dr)
        spatial_setup(C, dr)
        S.barrier()
        C.gs_ready = bias_setup(C, dr)
        S.barrier()
        g2tiles = [(1024, 256), (1280, 256), (2560, 256), (2816, 256)]
        alltiles = [(256 * t, 256) for t in range(12)]
        for blk in range(2):
            layer0(C, dr, dr["xTh"][blk], memkv_mode=("compute" if blk == 0 else "load"), next_norm=(0, 1))
            kv_proj(C, dr, g2tiles if blk == 0 else alltiles, dr["KVh"], blk * 1024, TBH)
            S.barrier()
        layer0(C, dr, dr["xT"], memkv_mode="load", next_norm=(0, 1))
        store_B(C, dr["xmid"], [])
        S.barrier()
        kv_proj(C, dr, alltiles, dr["o_kvT"], 0, TB)
        S.barrier()
        layer1(C, dr, dr["xmid"], memkv_key="o_memkvT1", preloaded=True)
        store_B(C, dr["o_y"], [])
        S.barrier()
        with nc.Block() as block:
            S.emit(block)
    return nc


def inputs_F(inp):
    xp = np.asarray(inp["x_prompt"], dtype=np.float32)
    xs = np.asarray(inp["x_sample"], dtype=np.float32)
    mem = np.asarray(inp["mem_prompt"], dtype=np.float32)
    cmem = np.asarray(inp["cache_mem_kv"], dtype=np.float32)
    caches = [np.asarray(inp[n], dtype=np.float32)[0] for n in ("cache_win128_kv", "cache_win512_kv", "cache_win2048_kv")]
    Soh, vmask = bias_tables()
    shared = common_inputs(inp)
    shared.update({
        "Jf": f32(np.eye(128, dtype=np.float32)[::-1]), "rel_bias": f32(inp["rel_bias"]), "Soh": Soh, "vmask": vmask,
        "w_mem_kv": f32(inp["w_mem_kv"]), "w_in_a": f32(np.asarray(inp["w_in_a"])[0]), "w_in_b": f32(np.asarray(inp["w_in_b"])[0]),
        "w_out": f32(inp["w_out"]), "w_ffn_up": f32(inp["w_ffn_up"]), "w_ffn_down": f32(inp["w_ffn_down"]),
    })
    in_maps = []
    for c in range(8):
        b, p = c // 4, c % 4
        s = p * 1024
        m = dict(shared)
        xtok = np.concatenate([xp[b, s:s + 1024], xs[c]], axis=0)
        m["xT"] = f32(xtok.T.reshape(16, 128, NT))
        xh = np.zeros((2, 1025, D), np.float32)
        for blk in range(2):
            t0 = s - 2048 + blk * 1024
            if t0 >= 0:
                xh[blk, :1024] = xp[b, t0:t0 + 1024]
        m["xTh"] = f32(xh.transpose(0, 2, 1).reshape(2, 16, 128, NT))
        m["memT"] = f32(mem[b].T.reshape(16, 128, 256))
        m["cmemKT"] = f32(cmem[:, c, :, 0].transpose(0, 2, 3, 1))
        m["cmemV"] = f32(cmem[:, c, :, 1].reshape(2, 256, 512))
        hvalid = np.zeros((128, 4), np.float32)
        hvalid[:, 0] = 1.0 if p >= 1 else 0.0
        hvalid[:64, 1] = 1.0 if p >= 2 else 0.0
        hvalid[64:, 1] = 1.0 if p >= 1 else 0.0
        hvalid[:, 2] = 1.0
        m["hvalid"] = hvalid
        m.update(cache_inputs(caches, c))
        in_maps.append(m)
    return in_maps


def run_F(inp):
    nc = build_F()
    res = run_bass_kernel_spmd(nc, inputs_F(inp), core_ids=list(range(8)))
    return res.results


def kernel_two(**inp):
    ra = run_A(inp)
    rb = run_B(inp, ra)
    return assemble(inp, ra, rb)


def kernel(**inp):
    r = run_F(inp)
    rb = [{"o_y": x["o_y"], "o_memkvT": x["o_memkvT1"]} for x in r]
    return assemble(inp, r, rb)
```
